# Optimizing a Trainium2 kernel written in Bass

```python
import math
import jax, jax.numpy as jnp
from jax import lax
import numpy as np

D_MODEL = 2048
BATCH = 4
SEQ = 2048
DEPTH = 1
DEC_BATCH = 128
DEC_SEQ = 1
PAST_LEN = 16384
PAGE_SIZE = 128

POOL_WINDOWS = (2, 4, 8, 16)
N_POOL_GROUPS = len(POOL_WINDOWS)
POOL_GROUP_DIM = D_MODEL // 8
POOL_DIM = N_POOL_GROUPS * POOL_GROUP_DIM
POOL_BUF = max(POOL_WINDOWS) - 1
GDN_HEAD_DIM = 128
GDN_HEADS = D_MODEL // GDN_HEAD_DIM
GDN_DIM = GDN_HEADS * GDN_HEAD_DIM
QKV_DIM = 3 * GDN_DIM
CONV_WIDTH = 4
CONV_BUF = CONV_WIDTH - 1
CHUNK = 64
D_FF = -(-8 * D_MODEL // (3 * 256)) * 256
PLE_DIM = 256
EPS = 1e-6
POOL_OFF = 0
QKV_OFF = POOL_OFF + POOL_DIM
Z_OFF = QKV_OFF + QKV_DIM
A_OFF = Z_OFF + GDN_DIM
B_OFF = A_OFF + GDN_HEADS
GP_OFF = B_OFF + GDN_HEADS
GG_OFF = GP_OFF + D_MODEL
IN_COLS = GG_OFF + D_MODEL

kernel_name = "pool_gdn_gated_hybrid_step"


def rmsnorm(x, g):
    xf = x.astype(jnp.float32)
    y = xf * lax.rsqrt(jnp.mean(xf * xf, axis=-1, keepdims=True) + EPS)
    return (y * g.astype(jnp.float32)).astype(x.dtype)


def l2norm(x):
    xf = x.astype(jnp.float32)
    return xf * lax.rsqrt(jnp.sum(xf * xf, axis=-1, keepdims=True) + EPS)


def pool_mixer(u, prefix, start_pos, w_grp, scale):
    Bx, T, _ = u.shape
    ext = jnp.concatenate([prefix.astype(u.dtype), u], axis=1)
    ef = ext.astype(jnp.float32)
    cs = jnp.concatenate([jnp.zeros_like(ef[:, :1]), jnp.cumsum(ef, axis=1)], axis=1)
    end = cs[:, POOL_BUF + 1:POOL_BUF + 1 + T]
    pos = start_pos + jnp.arange(T)
    means = []
    for gi, w in enumerate(POOL_WINDOWS):
        sl = slice(gi * POOL_GROUP_DIM, (gi + 1) * POOL_GROUP_DIM)
        start = cs[:, POOL_BUF + 1 - w:POOL_BUF + 1 - w + T, sl]
        cnt = jnp.minimum(pos + 1, w).astype(jnp.float32)[None, :, None]
        means.append((end[..., sl] - start) / cnt)
    d = jnp.concatenate(means, axis=-1) - u.astype(jnp.float32)
    d = d.reshape(Bx, T, N_POOL_GROUPS, POOL_GROUP_DIM)
    y = jnp.einsum("btgc,gcd->btgd", d, w_grp.astype(jnp.float32)).reshape(Bx, T, POOL_DIM)
    y = y * scale.astype(jnp.float32)
    return y.astype(u.dtype), ext[:, -POOL_BUF:]


def short_conv(u, prefix, w):
    T = u.shape[1]
    ext = jnp.concatenate([prefix.astype(u.dtype), u], axis=1)
    y = ext[:, 0:T] * w[0]
    for i in range(1, CONV_WIDTH):
        y = y + ext[:, i:i + T] * w[i]
    return jax.nn.silu(y), ext[:, -CONV_BUF:]


def gated_delta_chunked(q, k, v, g, beta, s0):
    f32 = jnp.float32
    Bx, T, H, DK = q.shape
    DV = v.shape[-1]
    C = min(CHUNK, T)
    n = -(-T // C)
    pad = n * C - T

    def prep(a):
        a = jnp.pad(a.astype(f32), [(0, 0), (0, pad)] + [(0, 0)] * (a.ndim - 2))
        a = a.reshape((Bx, n, C) + a.shape[2:])
        return jnp.moveaxis(a, 3, 1)

    q, k, v, g, beta = prep(q), prep(k), prep(v), prep(g), prep(beta)
    gc = jnp.cumsum(g, axis=-1)
    causal = jnp.tril(jnp.ones((C, C), bool))
    strict = jnp.tril(jnp.ones((C, C), bool), -1)
    decay = jnp.exp(jnp.where(causal, gc[..., :, None] - gc[..., None, :], -jnp.inf))
    kb = k * beta[..., None]
    a_mat = jnp.where(strict, jnp.einsum("bhncd,bhnsd->bhncs", kb, k) * decay, 0.0)
    eg = jnp.exp(gc)
    rhs = jnp.concatenate([v * beta[..., None], kb * eg[..., None]], axis=-1)
    sol = lax.linalg.triangular_solve(a_mat + jnp.eye(C, dtype=f32), rhs,
                                      left_side=True, lower=True, unit_diagonal=True)
    u_val, w_dec = sol[..., :DV], sol[..., DV:]
    qk = jnp.where(causal, jnp.einsum("bhncd,bhnsd->bhncs", q, k) * decay, 0.0)
    q_dec = q * eg[..., None]
    k_tail = k * jnp.exp(gc[..., -1:] - gc)[..., None]
    g_last = jnp.exp(gc[..., -1])
    xs = tuple(jnp.moveaxis(a, 2, 0) for a in (u_val, w_dec, qk, q_dec, k_tail, g_last))

    def step(S, inp):
        u_c, w_c, qk_c, qd_c, kt_c, gl_c = inp
        v_new = u_c - jnp.einsum("bhcd,bhde->bhce", w_c, S)
        o = jnp.einsum("bhcd,bhde->bhce", qd_c, S) + jnp.einsum("bhcs,bhse->bhce", qk_c, v_new)
        S = S * gl_c[..., None, None] + jnp.einsum("bhcd,bhce->bhde", kt_c, v_new)
        return S, o

    S, o = lax.scan(step, s0.astype(f32), xs)
    o = jnp.moveaxis(o, 0, 2).reshape(Bx, H, n * C, DV)[:, :, :T]
    return jnp.transpose(o, (0, 2, 1, 3)), S


def _layer(x, p_l, pool_buf, conv_buf, ssm, start_pos,
           norm_mix, w_in, pool_w, pool_scale, conv_w, a_log, dt_bias, gdn_norm,
           w_pool_up, w_gdn_up, w_o, norm_ffn, w_gate_up, w_down, w_ple, w_ple_gate):
    f32 = jnp.float32
    Bx, T, _ = x.shape
    h = rmsnorm(x, norm_mix)
    proj = h @ w_in
    gate_pool = jax.nn.sigmoid(proj[..., GP_OFF:GP_OFF + D_MODEL])
    gate_gdn = jax.nn.sigmoid(proj[..., GG_OFF:GG_OFF + D_MODEL])
    pool_out, pool_new = pool_mixer(proj[..., POOL_OFF:POOL_OFF + POOL_DIM], pool_buf,
                                    start_pos, pool_w, pool_scale)
    qkv, conv_new = short_conv(proj[..., QKV_OFF:QKV_OFF + QKV_DIM], conv_buf, conv_w)
    qkv = qkv.reshape(Bx, T, 3, GDN_HEADS, GDN_HEAD_DIM)
    q = l2norm(qkv[:, :, 0]) * (GDN_HEAD_DIM ** -0.5)
    k = l2norm(qkv[:, :, 1])
    v = qkv[:, :, 2]
    beta = jax.nn.sigmoid(proj[..., B_OFF:B_OFF + GDN_HEADS].astype(f32))
    g = -jnp.exp(a_log.astype(f32)) * jax.nn.softplus(
        proj[..., A_OFF:A_OFF + GDN_HEADS].astype(f32) + dt_bias.astype(f32))
    o, ssm_new = gated_delta_chunked(q, k, v, g, beta, ssm)
    z = proj[..., Z_OFF:Z_OFF + GDN_DIM].reshape(Bx, T, GDN_HEADS, GDN_HEAD_DIM).astype(f32)
    o = rmsnorm(o, gdn_norm) * jax.nn.silu(z)
    gdn_out = o.reshape(Bx, T, GDN_DIM).astype(x.dtype)
    merged = gate_pool * (pool_out @ w_pool_up) + gate_gdn * (gdn_out @ w_gdn_up)
    x = x + merged @ w_o
    gu = rmsnorm(x, norm_ffn) @ w_gate_up
    x = x + (jax.nn.silu(gu[..., :D_FF]) * gu[..., D_FF:]) @ w_down
    x = x + (p_l @ w_ple) * jax.nn.sigmoid(x @ w_ple_gate)
    return x, pool_new, conv_new, ssm_new.astype(x.dtype)


def _trunk(x, p, pool_st, conv_st, ssm_st, start_pos, layer_w, norm_final):
    pools, convs, ssms = [], [], []
    for i in range(DEPTH):
        x, pb, cb, sb = _layer(x, p[i], pool_st[i], conv_st[i], ssm_st[i], start_pos,
                               *[w[i] for w in layer_w])
        pools.append(pb)
        convs.append(cb)
        ssms.append(sb)
    return rmsnorm(x, norm_final), jnp.stack(pools), jnp.stack(convs), jnp.stack(ssms)


def setup_inputs(seed: int = 0) -> dict:
    key = jax.random.key(seed)
    ks = jax.random.split(key, 24)
    nrm = lambda k, s, sc: jax.random.normal(k, s, jnp.float32) * sc
    L = DEPTH
    u = jax.random.uniform(ks[10], (L, GDN_HEADS), jnp.float32)
    dt = jnp.exp(u * (math.log(0.1) - math.log(1e-3)) + math.log(1e-3))
    return {
        "x_prompt": nrm(ks[0], (BATCH, SEQ, D_MODEL), 1.0),
        "x_sample": nrm(ks[1], (DEC_BATCH, DEC_SEQ, D_MODEL), 1.0),
        "p_prompt": nrm(ks[2], (L, BATCH, SEQ, PLE_DIM), 1.0),
        "p_sample": nrm(ks[3], (L, DEC_BATCH, DEC_SEQ, PLE_DIM), 1.0),
        "state_pool": nrm(ks[4], (L, DEC_BATCH, POOL_BUF, POOL_DIM), 1.0),
        "state_conv": nrm(ks[5], (L, DEC_BATCH, CONV_BUF, QKV_DIM), 1.0),
        "state_ssm": nrm(ks[6], (L, DEC_BATCH, GDN_HEADS, GDN_HEAD_DIM, GDN_HEAD_DIM), 0.1),
        "norm_mix": 1.0 + nrm(ks[7], (L, D_MODEL), 0.02),
        "w_in": nrm(ks[8], (L, D_MODEL, IN_COLS), D_MODEL ** -0.5),
        "pool_w": nrm(ks[9], (L, N_POOL_GROUPS, POOL_GROUP_DIM, POOL_GROUP_DIM), POOL_GROUP_DIM ** -0.5),
        "pool_scale": 1.0 + nrm(ks[11], (L, POOL_DIM), 0.02),
        "conv_w": nrm(ks[12], (L, CONV_WIDTH, QKV_DIM), CONV_WIDTH ** -0.5),
        "a_log": jnp.log(jax.random.uniform(ks[13], (L, GDN_HEADS), jnp.float32, 1.0, 16.0)),
        "dt_bias": dt + jnp.log(-jnp.expm1(-dt)),
        "gdn_norm": 1.0 + nrm(ks[14], (L, GDN_HEAD_DIM), 0.02),
        "w_pool_up": nrm(ks[15], (L, POOL_DIM, D_MODEL), POOL_DIM ** -0.5),
        "w_gdn_up": nrm(ks[16], (L, GDN_DIM, D_MODEL), GDN_DIM ** -0.5),
        "w_o": nrm(ks[17], (L, D_MODEL, D_MODEL), D_MODEL ** -0.5),
        "norm_ffn": 1.0 + nrm(ks[18], (L, D_MODEL), 0.02),
        "w_gate_up": nrm(ks[19], (L, D_MODEL, 2 * D_FF), D_MODEL ** -0.5),
        "w_down": nrm(ks[20], (L, D_FF, D_MODEL), D_FF ** -0.5),
        "w_ple": nrm(ks[21], (L, PLE_DIM, D_MODEL), PLE_DIM ** -0.5),
        "w_ple_gate": nrm(ks[22], (L, D_MODEL, D_MODEL), D_MODEL ** -0.5),
        "norm_final": 1.0 + nrm(ks[23], (D_MODEL,), 0.02),
    }


def reference(x_prompt, x_sample, p_prompt, p_sample, state_pool, state_conv, state_ssm,
              norm_mix, w_in, pool_w, pool_scale, conv_w, a_log, dt_bias, gdn_norm,
              w_pool_up, w_gdn_up, w_o, norm_ffn, w_gate_up, w_down, w_ple, w_ple_gate,
              norm_final):
    layer_w = (norm_mix, w_in, pool_w, pool_scale, conv_w, a_log, dt_bias, gdn_norm,
               w_pool_up, w_gdn_up, w_o, norm_ffn, w_gate_up, w_down, w_ple, w_ple_gate)
    dt_ = x_prompt.dtype
    zero_pool = jnp.zeros((DEPTH, BATCH, POOL_BUF, POOL_DIM), dt_)
    zero_conv = jnp.zeros((DEPTH, BATCH, CONV_BUF, QKV_DIM), dt_)
    zero_ssm = jnp.zeros((DEPTH, BATCH, GDN_HEADS, GDN_HEAD_DIM, GDN_HEAD_DIM), dt_)
    y_prompt, pool_p, conv_p, ssm_p = _trunk(x_prompt, p_prompt, zero_pool, zero_conv, zero_ssm,
                                             0, layer_w, norm_final)
    y_sample, pool_s, conv_s, ssm_s = _trunk(x_sample, p_sample, state_pool, state_conv, state_ssm,
                                             PAST_LEN, layer_w, norm_final)
    return (y_prompt, y_sample, pool_p, conv_p, ssm_p, pool_s, conv_s, ssm_s)
```

```python
import numpy as np
from contextlib import ExitStack
import concourse.bass as bass
import concourse.mybir as mybir
from concourse.bass_utils import run_bass_kernel_spmd

F32 = mybir.dt.float32
BF16 = mybir.dt.bfloat16
AF = mybir.ActivationFunctionType
ALU = mybir.AluOpType

ENGS = ("pe", "act", "dve", "pool", "sp")

D = 2048
NT = 1024
NS = 16
NTOK = NT + NS
KC = 16
POOL_OFF, QKV_OFF, Z_OFF, A_OFF, GP_OFF, GG_OFF = 0, 1024, 7168, 9216, 9248, 11296
IN_COLS = 13344
DFF = 5632
EPS = 1e-6
BIG = 32768.0
WINS = (2, 4, 8, 16)
TG = ((0, 512), (512, 512), (1024, 16))
TGP = ((0, 512), (512, 512))


class Ins:
    __slots__ = ("eng", "fn", "reads", "writes", "dsem", "needs_inc", "deps", "sig", "idx")

    def __init__(self, eng, fn, reads, writes, dsem):
        self.eng = eng
        self.fn = fn
        self.reads = tuple(reads)
        self.writes = tuple(writes)
        self.dsem = dsem
        self.needs_inc = False
        self.deps = []
        self.sig = None


class Prog:
    def __init__(self, nc):
        self.nc = nc
        self.ins = []

    def add(self, eng, fn, reads=(), writes=(), dsem=None):
        self.ins.append(Ins(eng, fn, reads, writes, dsem))

    def pe(self, fn, r=(), w=()):
        self.add("pe", fn, r, w)

    def act(self, fn, r=(), w=()):
        self.add("act", fn, r, w)

    def dve(self, fn, r=(), w=()):
        self.add("dve", fn, r, w)

    def pool(self, fn, r=(), w=()):
        self.add("pool", fn, r, w)

    def dma(self, eng, dsem, fn, r=(), w=()):
        self.add(eng, fn, r, w, dsem=dsem)

    def analyze(self):
        last_w, rd_eng, rd_dma = {}, {}, {}
        for i, I in enumerate(self.ins):
            I.idx = i
            deps = {}
            for k in I.reads:
                d = last_w.get(k)
                if d is not None:
                    deps[d] = "raw"
            for k in I.writes:
                d = last_w.get(k)
                if d is not None and d not in deps:
                    deps[d] = "waw"
                for d in rd_eng.get(k, {}).values():
                    if d not in deps:
                        deps[d] = "war"
                for d in rd_dma.get(k, ()):
                    if d not in deps:
                        deps[d] = "war"
            for d, kind in deps.items():
                Dn = self.ins[d]
                if Dn.dsem is None and Dn.eng == I.eng and I.dsem is None:
                    if I.eng == "pe" or kind == "war":
                        continue
                I.deps.append(d)
                if Dn.dsem is None:
                    Dn.needs_inc = True
            for k in I.reads:
                if I.dsem is not None:
                    rd_dma.setdefault(k, []).append(i)
                else:
                    rd_eng.setdefault(k, {})[I.eng] = i
            for k in I.writes:
                last_w[k] = i
                rd_eng[k] = {}
                rd_dma[k] = []
        cnt = {e: 0 for e in ENGS}
        dcnt = {}
        for I in self.ins:
            if I.dsem is not None:
                dcnt[I.dsem] = dcnt.get(I.dsem, 0) + 16
                I.sig = (("d", I.dsem), dcnt[I.dsem])
            elif I.needs_inc:
                cnt[I.eng] += 1
                I.sig = (("e", I.eng), cnt[I.eng])
        self.final_counts = (cnt, dcnt)

    def emit(self, final_eng="sp"):
        nc = self.nc
        self.analyze()
        with ExitStack() as es:
            esem = {e: es.enter_context(nc.semaphore("c_" + e)) for e in ENGS}
            dnames = sorted({I.dsem for I in self.ins if I.dsem is not None})
            dsem = {n: es.enter_context(nc.semaphore("d_" + n)) for n in dnames}
            block = es.enter_context(nc.Block())

            def semof(key):
                return esem[key[1]] if key[0] == "e" else dsem[key[1]]

            def run_engine(ename, h):
                known = {}
                for I in self.ins:
                    if I.eng != ename:
                        continue
                    need = {}
                    for d in I.deps:
                        key, val = self.ins[d].sig
                        if need.get(key, 0) < val:
                            need[key] = val
                    for key, val in need.items():
                        if known.get(key, 0) >= val:
                            continue
                        h.wait_ge(semof(key), val)
                        known[key] = val
                    bi = I.fn(h)
                    if I.dsem is not None:
                        bi.then_inc(dsem[I.dsem], 16)
                    elif I.needs_inc:
                        bi.then_inc(esem[ename], 1)
                if ename == final_eng:
                    cnt, dcnt = self.final_counts
                    for n, v in dcnt.items():
                        if known.get(("d", n), 0) < v:
                            h.wait_ge(dsem[n], v)
                    for e, v in cnt.items():
                        if v > 0 and e != ename and known.get(("e", e), 0) < v:
                            h.wait_ge(esem[e], v)

            @block.sync
            def _(h):
                run_engine("sp", h)

            @block.tensor
            def _(h):
                run_engine("pe", h)

            @block.scalar
            def _(h):
                run_engine("act", h)

            @block.vector
            def _(h):
                run_engine("dve", h)

            @block.gpsimd
            def _(h):
                run_engine("pool", h)


W_SHAPES = {
    "norm_mix": [1, D], "w_in": [1, D, IN_COLS], "pool_w": [1, 4, 256, 256], "pool_scale": [1, 1024],
    "conv_w": [1, 4, 6144], "a_log": [1, 16], "dt_bias": [1, 16], "gdn_norm": [1, 128],
    "w_pool_up": [1, 1024, D], "w_gdn_up": [1, D, D], "w_o": [1, D, D], "norm_ffn": [1, D],
    "w_gate_up": [1, D, 2 * DFF], "w_down": [1, DFF, D], "w_ple": [1, 256, D], "w_ple_gate": [1, D, D],
    "norm_final": [D],
}
IN_SHAPES = {
    "xo": [NT, D], "xp": [NT, D], "xs": [NS, D], "po": [NT, 256], "psm": [NS, 256],
    "st_pool": [NS, 15, 1024], "st_conv": [NS, 3, 6144], "st_ssm": [NS, 16, 128, 128],
    "c_ident": [128, 128], "c_utri": [128, 128], "c_hm": [128, 256], "c_bo": [128, 128], "c_ma": [128, 128], "c_mb": [128, 128],
    "c_invcnt": [1, 64], "c_selp": [120, 128], "c_selc": [48, 16], "c_seli": [16, 256],
    "l_cw": [128, 192], "l_pscale": [128, 8], "l_gnw": [128, 1], "l_nw": [128, 32],
}
OUT_SHAPES = {
    "y_o": [NT, D], "y_s": [NS, D], "poolT_o": [1024, 15], "convT_o": [6144, 3], "ssm_o": [16, 128, 128],
    "pool_s_rows": [NS, 14, 1024], "pool_s_lastT": [1024, NS], "conv_s_rows": [NS, 2, 6144],
    "conv_s_lastT": [6144, NS], "ssm_s": [NS, 16, 128, 128],
}


class _Stop(Exception):
    pass


def build(debug=(), stop=None):
    nc = bass.Bass("TRN2", target_bir_lowering=False)
    dr = {}
    for n, s in list(W_SHAPES.items()) + list(IN_SHAPES.items()):
        dr[n] = nc.dram_tensor(n, s, F32, kind="ExternalInput").ap()
    for n, s in OUT_SHAPES.items():
        dr[n] = nc.dram_tensor(n, s, F32, kind="ExternalOutput").ap()
    dbg_out = {}
    P = Prog(nc)
    es = ExitStack()
    with es:
        def sb(name, shape, dt=F32):
            return es.enter_context(nc.sbuf_tensor(name, shape, dt))

        def pst(name, shape, dt=F32):
            return es.enter_context(nc.psum_tensor(name, shape, dt))

        PSUM_NAMES = ("pm0", "pm1", "pm2", "pg", "ptr", "pq", "psn", "px")

        def bankkeys(*aps):
            ks = []
            for a_ in aps:
                n_ = getattr(a_, "name", None)
                if n_ in PSUM_NAMES and ("bank", n_) not in ks:
                    ks.append(("bank", n_))
            return ks

        def MM(out, lhsT, rhs, r, w, start=True, stop=True):
            P.pe(lambda h: h.matmul(out, lhsT=lhsT, rhs=rhs, start=start, stop=stop), r, list(w) + bankkeys(out))

        def TR(out, in_, idn, r, w):
            P.pe(lambda h: h.transpose(out, in_, idn), r, list(w) + bankkeys(out))

        def ACT(out, in_, func, r, w, **kw):
            P.act(lambda h: h.activation(out=out, in_=in_, func=func, **kw), r, list(w) + bankkeys(out, in_))

        def TS(out, in0, s1, s2, op0, op1, r, w, eng="dve"):
            w = list(w) + bankkeys(out, in0)
            if s2 is None:
                P.add(eng, lambda h: h.tensor_scalar(out=out, in0=in0, scalar1=s1, scalar2=None, op0=op0), r, w)
            else:
                P.add(eng, lambda h: h.tensor_scalar(out=out, in0=in0, scalar1=s1, scalar2=s2, op0=op0, op1=op1), r, w)

        def STT(out, in0, scalar, in1, op0, op1, r, w):
            P.dve(lambda h: h.scalar_tensor_tensor(out=out, in0=in0, scalar=scalar, in1=in1, op0=op0, op1=op1), r,
                  list(w) + bankkeys(out, in0, in1))

        def TT(out, in0, in1, op, r, w, eng="dve"):
            P.add(eng, lambda h: h.tensor_tensor(out=out, in0=in0, in1=in1, op=op), r, list(w) + bankkeys(out, in0, in1))

        def CP(out, in_, r, w, eng="dve"):
            P.add(eng, lambda h: h.tensor_copy(out=out, in_=in_), r, list(w) + bankkeys(out, in_))

        def RECIP(out, in_, r, w):
            P.dve(lambda h: h.reciprocal(out=out, in_=in_), r, w)

        def MEMSET(ap_, val, w):
            P.dve(lambda h: h.memset(ap_, val), (), w)

        def LD(dsem, out, in_, w, r=(), eng="sp"):
            P.dma(eng, dsem, lambda h: h.dma_start(out=out, in_=in_), r, w)

        def ST(dsem, out, in_, r, eng="sp"):
            P.dma(eng, dsem, lambda h: h.dma_start(out=out, in_=in_), r, ())

        def dump(name, ap_, rkeys):
            if name in debug:
                o = nc.dram_tensor("dbg_" + name, list(ap_.shape), ap_.dtype, kind="ExternalOutput").ap()
                dbg_out[name] = o
                ST("dbg_" + name, o, ap_, rkeys)

        ident = sb("ident", [128, 128]); identb = sb("identb", [128, 128], BF16)
        utri = sb("utri", [128, 128]); onesf = sb("onesf", [128, 128]); onesb = sb("onesb", [128, 128], BF16)
        mab = sb("mab", [128, 128], BF16); mbb = sb("mbb", [128, 128], BF16); mcb = sb("mcb", [128, 128], BF16)
        mstage = sb("mstage", [128, 256])
        halfm = sb("halfm", [128, 256]); blkones = sb("blkones", [128, 128])
        invcnt = sb("invcnt", [128, 64])
        nwfm = sb("nwfm", [128, 2, 16])
        cw = sb("cw", [128, 48, 4])
        pscale = sb("pscale", [128, 8])
        gnw = sb("gnw", [128, 1])
        dtb = sb("dtb", [128, 16]); nA = sb("nA", [128, 16])
        bar = sb("bar", [128, 2]); epsc = sb("epsc", [128, 1])
        hT = sb("hT", [128, KC, NTOK], BF16)
        bufM = sb("bufM", [128, 16, NTOK], BF16)
        pT = bufM[:, 12:14, :]
        NWB = 2
        wb = [sb("wb%d" % i, [128, 8192], BF16) for i in range(NWB)]
        BIGN = 20736 + 22272
        big = sb("big", [128, BIGN], BF16)
        xres = big[:, 0:36864].bitcast(F32).rearrange("p (t d) -> p t d", t=9)

        def carver(base):
            st_ = [base]

            def carve(nelem_bf16, dt=BF16):
                a_ = st_[0]
                st_[0] += nelem_bf16
                assert st_[0] <= BIGN, (st_[0], BIGN)
                v = big[:, a_:a_ + nelem_bf16]
                return v.bitcast(F32) if dt == F32 else v
            return carve
        c0_ = carver(0)
        gdnT = c0_(16 * NTOK).rearrange("p (k t) -> p k t", k=16)
        Sst = c0_(2 * 16 * 128, F32).rearrange("p (h e) -> p h e", h=16)
        UB = 20736
        cA = carver(UB)
        pre = cA(2 * (NTOK + 8), F32); acc = cA(2 * NTOK, F32)
        qTb = [cA(NTOK), cA(NTOK)]; kTb = [cA(NTOK), cA(NTOK)]; vTb = [cA(NTOK), cA(NTOK)]; szTb = [cA(NTOK), cA(NTOK)]
        rinv = cA(1024, F32); Sbf = cA(128)
        Oh = cA(2 * 8 * 128, F32).rearrange("p (t e) -> p t e", t=8)
        NB = 2
        ktl = [[cA(128) for _ in range(NB)] for _ in range(2)]; vbt = [[cA(128) for _ in range(NB)] for _ in range(2)]
        QKd = [[cA(128) for _ in range(NB)] for _ in range(2)]
        Es = [cA(256, F32)]; EiT = [cA(256, F32)]; Eo = cA(256, F32)
        Cb = [[cA(128) for _ in range(NB)] for _ in range(2)]
        PP = [[[cA(256) for _ in range(NB)] for _ in range(2)] for _ in range(2)]
        TTm = [[[cA(128) for _ in range(NB)] for _ in range(2)] for _ in range(2)]
        tmpb = cA(128); vnew = cA(128); Bc = cA(256, F32); onb = cA(128)
        cX = carver(UB)
        xin = [cX(4096, F32)]
        cP = carver(UB)
        Uext = cP(2 * (NTOK + 16), F32); s1 = cP(2 * (NTOK + 16), F32); s2 = cP(2 * (NTOK + 16), F32)
        DT = cP(2 * NTOK).rearrange("p (k t) -> p k t", k=2)
        PO = cP(8 * NTOK).rearrange("p (k t) -> p k t", k=8)
        stp = cP(2 * 2 * 128, F32).rearrange("p (h c) -> p h c", h=2)
        cS = carver(UB)
        Ssm = [cS(4096, F32).rearrange("p (h e) -> p h e", h=16) for _ in range(2)]
        qkS = cS(2 * 16 * NS * 2, F32).rearrange("p (h i t) -> p h i t", h=16, i=NS)
        vS = cS(2 * NS * 16, F32).rearrange("p (i h) -> p i h", i=NS)
        oS = cS(2 * NS * 16, F32).rearrange("p (i h) -> p i h", i=NS)
        bcS = cS(2 * 3 * 256, F32).rearrange("p (a n) -> p a n", a=3)
        smpW = cS(2 * 4 * 256, F32).rearrange("p (a n) -> p a n", a=4)
        Stm = cS(256, F32); Stm2 = [Stm, cS(256, F32)]
        stc = cS(2 * 512, F32); wrep = cS(2 * 512, F32)
        seli = cS(2 * 256, F32); xd = cS(2 * 256, F32).rearrange("p (i h) -> p i h", i=16)
        UNION_KEYS = (["pre", "acc", "xn", "junk", "gsig", "gmul", "sq", "rinv", "Sbf", "tmpb", "vnew", "Bc", "onb", "Es0", "EiT0", "Eo", ("Cb", 0, 0), ("Cb", 0, 1), ("Cb", 1, 0), ("Cb", 1, 1), ("qT", 0), ("qT", 1), ("kT", 0), ("kT", 1), ("vT", 0), ("vT", 1),
                       ("szT", 0), ("szT", 1), ("xin", 0), "Uext", "s1", "s2",
                       "stp", ("DT", 0), ("DT", 1), "qkS", "vS", "oS", "smpW0", "smpW1", "smpW2", "smpW3", ("Stm", 0), ("Stm", 1),
                       "stc", "seli", "xd"] + [("Oh", t) for t in range(8)] + [("ss", t) for t in range(8)] +
                      [(n_, s_, i_) for n_ in ("ktl", "vbt", "QKd", "PP0a", "PP0b", "PP1a", "PP1b", "TT0", "TT1") for s_ in range(2)
                       for i_ in range(2)] +
                      [("PO", c) for c in range(8)] + [("bcS", a_) for a_ in range(3)] + [("wrep", i) for i in range(NS)] + [("Ssm", a_, h_) for a_ in range(2) for h_ in range(16)])
        gtok = sb("gtok", [128, 9, 16]); btok = sb("btok", [128, 9, 16])
        gcs = sb("gcs", [128, 9, 16]); egs = sb("egs", [128, 9, 16]); ets = sb("ets", [128, 9, 16])
        gls = sb("gls", [128, 9, 2, 16]); nbs = sb("nbs", [128, 9, 16]); nbegs = sb("nbegs", [128, 9, 16])
        ngcs = sb("ngcs", [128, 9, 16])
        tmp16 = sb("tmp16", [128, 4, 16])
        convtail = sb("convtail", [128, 48, 3]); pooltail = sb("pooltail", [128, 8, 15])
        stat = sb("stat", [128, 16]); stat2 = sb("stat2", [128, 16])
        xn = sb("xn", [128, D], BF16)
        sq = xn[:, 0:512]; junk = xn[:, 1024:2048]
        pin = sb("pin", [128, 256])
        wab = sb("wab", [128, KC, 32], BF16)
        szS = sb("szS", [128, 16, NS])
        sm16 = sb("sm16", [128, 8, 32])
        selp = sb("selp", [120, 128]); selc = sb("selc", [48, 16])
        upool_s = sb("upool_s", [128, 8, NS]); preS = sb("preS", [128, 48, NS])

        def barrier():
            P.dve(lambda h: h.memset(bar[:, 0:1], 0.0), (), list(UNION_KEYS))

        def ckpt(name):
            if stop == name:
                raise _Stop()

        pm = [pst("pm%d" % i, [128, 512]) for i in range(3)]
        pg = pst("pg", [128, 512])
        ptr = pst("ptr", [128, 1024], BF16)
        pq = pst("pq", [128, 512])
        psn = pst("psn", [128, 512])
        px = pst("px", [128, 512])
        tbanks = [[pg, pq], [pg, pq]]
        pm_i = [0]

        def next_pm():
            i = pm_i[0] % 3
            pm_i[0] += 1
            return pm[i], ("pm", i)

        wslot_i = [0]

        def wkeys(s_):
            return [("wb", s_, q) for q in range(4)]

        def wload(src_ap, kc, ncols):
            s_ = wslot_i[0] % NWB
            wslot_i[0] += 1
            v = wb[s_][:, 0:kc * ncols].rearrange("p (k c) -> p k c", k=kc)
            P.dma("pool", "wb%d" % s_, lambda h: h.dma_start(out=v, in_=src_ap), (), wkeys(s_))
            return v, wkeys(s_)

        def wload2(srcA, kcA, ncA, srcB, kcB, ncB):
            s_ = wslot_i[0] % NWB
            wslot_i[0] += 1
            assert kcA * ncA <= 4096 and kcB * ncB <= 4096
            vA = wb[s_][:, 0:kcA * ncA].rearrange("p (k c) -> p k c", k=kcA)
            vB = wb[s_][:, 4096:4096 + kcB * ncB].rearrange("p (k c) -> p k c", k=kcB)
            P.dma("pool", "wb%d_0" % s_, lambda h: h.dma_start(out=vA, in_=srcA), (), wkeys(s_))
            P.dma("pool", "wb%d_1" % s_, lambda h: h.dma_start(out=vB, in_=srcB), (), [("wb", s_, 2), ("wb", s_, 3)])
            return vA, [("wb", s_, 0), ("wb", s_, 1)], vB, [("wb", s_, 2), ("wb", s_, 3)]

        def wrows(name, r0, kc, c0, ncols):
            a = dr[name][0] if len(W_SHAPES[name]) == 3 else dr[name]
            return a[r0:r0 + kc * 128, c0:c0 + ncols].rearrange("(k p) c -> p k c", p=128)

        LD("c0", ident[:], dr["c_ident"], ["ident"])
        LD("c1", utri[:], dr["c_utri"], ["utri"])
        LD("c2", mstage[:, 0:128], dr["c_ma"], ["mst0"])
        LD("c2b", mstage[:, 128:256], dr["c_mb"], ["mst1"])
        LD("c13", halfm[:], dr["c_hm"], ["halfm"])
        LD("c14", blkones[:], dr["c_bo"], ["blkones"])
        LD("c3", invcnt[:], dr["c_invcnt"].partition_broadcast(128), ["invcnt"])
        LD("c4", cw[:].rearrange("p k t -> p (k t)"), dr["l_cw"], ["cw"])
        LD("c5", pscale[:], dr["l_pscale"], ["pscale"])
        LD("c6", gnw[:], dr["l_gnw"], ["gnw"])
        LD("c7", dtb[:], dr["dt_bias"].partition_broadcast(128), ["dtb"])
        LD("c8", nA[:], dr["a_log"].partition_broadcast(128), ["nA"])
        LD("c9", selp[:], dr["c_selp"], ["selp"])
        LD("c10", selc[:], dr["c_selc"], ["selc"])
        P.dma("pool", "wab", lambda h: h.dma_start(out=wab[:], in_=wrows("w_in", 0, KC, A_OFF, 32)), (), ["wab"])
        LD("c12", nwfm[:].rearrange("p a k -> p (a k)"), dr["l_nw"], ["nwfm"])
        CP(identb[:], ident[:], ["ident"], ["identb"])
        CP(mab[:], mstage[:, 0:128], ["mst0"], ["mab"])
        CP(mbb[:], mstage[:, 128:256], ["mst1"], ["mbb"])
        CP(mcb[:], halfm[:, 0:128], ["halfm"], ["mcb"])
        MEMSET(onesf[:], 1.0, ["onesf"])
        MEMSET(onesb[:], 1.0, ["onesb"])
        MEMSET(epsc[:], EPS, ["epsc"])
        ACT(nA[:], nA[:], AF.Exp, ["nA"], ["nA"])
        TS(nA[:], nA[:], -1.0, None, ALU.mult, None, ["nA"], ["nA"])

        def transposes_to_hT(npart, t, tcol0, which):
            for k4 in range(4):
                for kk in range(4):
                    k = k4 * 4 + kk
                    TR(ptr[:, kk * 128:kk * 128 + npart], xn[0:npart, k * 128:(k + 1) * 128], identb[0:npart, 0:npart],
                       ["xn", "identb"], [("ptr", kk)])
                dst = hT[:, k4 * 4:k4 * 4 + 4, tcol0:tcol0 + npart]
                src = ptr[:, 0:512].rearrange("p (k t) -> p k t", k=4)[:, :, 0:npart]
                rk = [("ptr", kk) for kk in range(4)]
                if which is None:
                    if k4 % 2 == 0:
                        ACT(dst, src, AF.Copy, rk, [("hT", t, k4)])
                    else:
                        CP(dst, src, rk, [("hT", t, k4)])
                else:
                    TT(dst, src, nwfm[:, which, k4 * 4:k4 * 4 + 4].unsqueeze(2).to_broadcast([128, 4, npart]), ALU.mult,
                       rk + ["nwfm"], [("hT", t, k4)])

        def rstd_of(xt, npart, xkeys):
            ACT(xn[0:npart, :], xt, AF.Square, xkeys, ["xn", "stat"], accum_out=stat[0:npart, 0:1])
            TS(stat[0:npart, 1:2], stat[0:npart, 0:1], 1.0 / D, EPS, ALU.mult, ALU.add, ["stat"], ["stat1"])
            ACT(stat[0:npart, 2:3], stat[0:npart, 1:2], AF.Sqrt, ["stat1"], ["stat2"])
            RECIP(stat[0:npart, 3:4], stat[0:npart, 2:3], ["stat2"], ["stat3"])

        def norm_tile(xt, npart, t, tcol0, xkeys, which):
            rstd_of(xt, npart, xkeys)
            ACT(xn[0:npart, :], xt, AF.Copy, list(xkeys) + ["stat3"], ["xn"], scale=stat[0:npart, 3:4])
            transposes_to_hT(npart, t, tcol0, which)

        def load_x_tile(src_ap, npart):
            LD("xin0", xin[0][0:npart, :], src_ap, [("xin", 0)])
            return xin[0][0:npart, :], ("xin", 0)

        def hkeys(tl):
            return [("hT", t, k4) for t in tl for k4 in range(4)]

        def ab_tile(tile, tcol0, npart):
            for k in range(KC):
                MM(px[0:npart, 0:32], hT[:, k, tcol0:tcol0 + npart], wab[:, k, :], hkeys([tile]) + ["wab"], ["px_ab"],
                   start=(k == 0), stop=(k == KC - 1))
            ACT(tmp16[0:npart, 0, :], px[0:npart, 16:32], AF.Exp, ["px_ab"], ["t16_0"], scale=-1.0)
            TS(tmp16[0:npart, 0, :], tmp16[0:npart, 0, :], 1.0, None, ALU.add, None, ["t16_0"], ["t16_0"])
            RECIP(btok[0:npart, tile, :], tmp16[0:npart, 0, :], ["t16_0"], [("btok", tile)])
            TT(tmp16[0:npart, 1, :], px[0:npart, 0:16], dtb[0:npart, :], ALU.add, ["px_ab", "dtb"], ["t16_1"])
            ACT(tmp16[0:npart, 1, :], tmp16[0:npart, 1, :], AF.Exp, ["t16_1"], ["t16_1"])
            ACT(tmp16[0:npart, 1, :], tmp16[0:npart, 1, :], AF.Ln, ["t16_1"], ["t16_1"], bias=1.0)
            TT(gtok[0:npart, tile, :], tmp16[0:npart, 1, :], nA[0:npart, :], ALU.mult, ["t16_1", "nA"], [("gtok", tile)])
            TS(nbs[0:npart, tile, :], btok[0:npart, tile, :], -1.0, None, ALU.mult, None, [("btok", tile)], [("nbs", tile)])
            if npart == 128:
                MM(px[:, 32:48], utri[:], gtok[:, tile, :], ["utri", ("gtok", tile)], ["px_gc"])
                MM(px[:, 48:64], onesf[:], gtok[:, tile, :], ["onesf", ("gtok", tile)], ["px_gl"])
                CP(gcs[:, tile, :], px[:, 32:48], ["px_gc"], [("gcs", tile)])
                TS(ngcs[:, tile, :], px[:, 32:48], -1.0, None, ALU.mult, None, ["px_gc"], [("ngcs", tile)])
                ACT(egs[:, tile, :], px[:, 32:48], AF.Exp, ["px_gc"], [("egs", tile)])
                ACT(gls[:, tile, 0, :], px[:, 48:64], AF.Exp, ["px_gl"], [("gls", tile)])
                TT(tmp16[:, 2, :], px[:, 48:64], gcs[:, tile, :], ALU.subtract, ["px_gl", ("gcs", tile)], ["t16_2"])
                ACT(ets[:, tile, :], tmp16[:, 2, :], AF.Exp, ["t16_2"], [("ets", tile)])
                TT(nbegs[:, tile, :], nbs[:, tile, :], egs[:, tile, :], ALU.mult, [("nbs", tile), ("egs", tile)],
                   [("nbegs", tile)])
            else:
                ACT(egs[0:npart, tile, :], gtok[0:npart, tile, :], AF.Exp, [("gtok", tile)], [("egs", tile)])

        def gkeys(tile):
            return [(n, tile) for n in ("gtok", "gcs", "ngcs", "egs", "ets", "gls", "nbs", "nbegs", "btok")]

        def proj_chunk(wv, wk, wcol, groups, hk, consume):
            for (g0, gn) in groups:
                ps_, pk = next_pm()
                for k in range(KC):
                    MM(ps_[:, 0:gn], wv[:, k, wcol:wcol + 128], hT[:, k, g0:g0 + gn], hk + wk, [pk],
                       start=(k == 0), stop=(k == KC - 1))
                consume(ps_[:, 0:gn], pk, g0, gn)

        def conv_chunk(chunk, n, dst, dkey, l2scale):
            w4 = cw[:, chunk, :]
            ACT(acc[:, 0:n], pre[:, 0:n], AF.Copy, ["pre", "cw"], ["acc"], scale=w4[:, 0:1])
            for t in (1, 2, 3):
                STT(acc[:, 0:n], pre[:, t:t + n], w4[:, t:t + 1], acc[:, 0:n], ALU.mult, ALU.add, ["pre", "acc", "cw"], ["acc"])
            if l2scale is None:
                ACT(dst[:, 0:n], acc[:, 0:n], AF.Silu, ["acc"], [dkey])
                return
            ACT(acc[:, 0:n], acc[:, 0:n], AF.Silu, ["acc"], ["acc"])
            for g0 in range(0, n, 512):
                gn = min(512, n - g0)
                ACT(sq[:, 0:gn], acc[:, g0:g0 + gn], AF.Square, ["acc"], ["sq"])
                ps_, pk = next_pm()
                MM(ps_[:, 0:gn], onesb[:], sq[:, 0:gn], ["onesb", "sq"], [pk])
                ACT(rinv[:, 0:gn], ps_[:, 0:gn], AF.Ln, [pk, "epsc"], ["rinv"], bias=epsc[:, 0:1])
                ACT(rinv[:, 0:gn], rinv[:, 0:gn], AF.Exp, ["rinv"], ["rinv"], scale=-0.5)
                STT(dst[:, g0:g0 + gn], acc[:, g0:g0 + gn], l2scale, rinv[:, 0:gn], ALU.mult, ALU.mult,
                    ["acc", "rinv"], [dkey])

        def stageA(hd, b, full, par):
            kT = kTb[par]; vT = vTb[par]; qT = qTb[par]
            s_ = b % 2
            tb = tbanks[s_]
            tl = list(range(b * NB, b * NB + NB))
            for i, t in enumerate(tl):
                c0 = t * 128
                TR(ptr[:, (4 * s_ + 2 * i) * 128:(4 * s_ + 2 * i + 1) * 128], kT[:, c0:c0 + 128], identb[:], [("kT", par), "identb"],
                   [("ptr", 4 * s_ + 2 * i)])
                TR(ptr[:, (4 * s_ + 2 * i + 1) * 128:(4 * s_ + 2 * i + 2) * 128], vT[:, c0:c0 + 128], identb[:], [("vT", par), "identb"],
                   [("ptr", 4 * s_ + 2 * i + 1)])
            for i, t in enumerate(tl):
                gk = gkeys(t)
                pk_ = 4 * s_ + 2 * i
                ACT(ktl[s_][i][:], ptr[:, pk_ * 128:(pk_ + 1) * 128], AF.Copy, [("ptr", pk_)] + gk, [("ktl", s_, i)],
                    scale=ets[:, t, hd:hd + 1])
                TS(vbt[s_][i][:], ptr[:, (pk_ + 1) * 128:(pk_ + 2) * 128], btok[:, t, hd:hd + 1], None, ALU.mult, None,
                   [("ptr", pk_ + 1)] + gk, [("vbt", s_, i)])
            for i, t in enumerate(tl):
                c0 = t * 128
                gk = gkeys(t)
                B_ = tb[i]
                gcol = gtok[:, t, hd:hd + 1].to_broadcast([128, 128])
                MM(B_[:, 0:128], kT[:, c0:c0 + 128], kT[:, c0:c0 + 128], [("kT", par)], [("tb", s_, i, 0)])
                MM(B_[:, 256:384], gcol, utri[:], gk + ["utri"], [("tb", s_, i, 2)], start=True, stop=False)
                MM(B_[:, 256:384], identb[:], mab[:], ["identb", "mab"], [("tb", s_, i, 2)], start=False, stop=True)
            for i, t in enumerate(tl):
                gk = gkeys(t)
                B_ = tb[i]
                e_ = 0
                ACT(Es[e_][:], B_[:, 256:384], AF.Exp, [("tb", s_, i, 2)] + gk, ["Es%d" % e_], scale=-1.0, bias=gcs[:, t, hd:hd + 1])
                STT(PP[s_][0][i][:, 0:128], B_[:, 0:128], nbs[:, t, hd:hd + 1], Es[e_][:], ALU.mult, ALU.mult,
                    [("tb", s_, i, 0), "Es%d" % e_] + gk, [("PP0a", s_, i)])
            for i, t in enumerate(tl):
                pk_ = 4 * s_ + i
                TR(ptr[:, pk_ * 128:(pk_ + 1) * 128], PP[s_][0][i][:, 0:128], identb[:], [("PP0a", s_, i), "identb"], [("ptr", pk_)])
            for i, t in enumerate(tl):
                pk_ = 4 * s_ + i
                ACT(PP[s_][0][i][:, 128:256], ptr[:, pk_ * 128:(pk_ + 1) * 128], AF.Copy, [("ptr", pk_)], [("PP0b", s_, i)])
                TT(TTm[s_][0][i][:], ptr[:, pk_ * 128:(pk_ + 1) * 128], identb[:], ALU.add, [("ptr", pk_), "identb"], [("TT0", s_, i)])
            for j in range(1, 6):
                a, b2 = (j - 1) % 2, j % 2
                for i, t in enumerate(tl):
                    B_ = tb[i]
                    ka = [("PP%da" % a, s_, i), ("PP%db" % a, s_, i)]
                    MM(B_[:, 0:128], PP[s_][a][i][:, 128:256], PP[s_][a][i][:, 0:128], ka, [("tb", s_, i, 0)])
                    if j < 5:
                        MM(B_[:, 128:256], PP[s_][a][i][:, 0:128], PP[s_][a][i][:, 128:256], ka, [("tb", s_, i, 1)])
                for i, t in enumerate(tl):
                    B_ = tb[i]
                    n_ = 256 if j < 5 else 128
                    wk_ = [("PP%da" % b2, s_, i), ("PP%db" % b2, s_, i)] if j < 5 else [("PP%da" % b2, s_, i)]
                    rk_ = [("tb", s_, i, 0), ("tb", s_, i, 1)] if j < 5 else [("tb", s_, i, 0)]
                    if j < 5:
                        if i % 2 == 0:
                            ACT(PP[s_][b2][i][:, 0:128], B_[:, 0:128], AF.Copy, rk_[0:1], wk_[0:1])
                            CP(PP[s_][b2][i][:, 128:256], B_[:, 128:256], rk_[1:2], wk_[1:2])
                        else:
                            CP(PP[s_][b2][i][:, 0:128], B_[:, 0:128], rk_[0:1], wk_[0:1])
                            ACT(PP[s_][b2][i][:, 128:256], B_[:, 128:256], AF.Copy, rk_[1:2], wk_[1:2])
                    elif i % 2 == 0:
                        ACT(PP[s_][b2][i][:, 0:n_], B_[:, 0:n_], AF.Copy, rk_, wk_)
                    else:
                        CP(PP[s_][b2][i][:, 0:n_], B_[:, 0:n_], rk_, wk_)
                for i, t in enumerate(tl):
                    B_ = tb[i]
                    MM(B_[:, 256:384], PP[s_][b2][i][:, 0:128], TTm[s_][a][i][:], [("PP%da" % b2, s_, i), ("TT%d" % a, s_, i)],
                       [("tb", s_, i, 2)])
                for i, t in enumerate(tl):
                    B_ = tb[i]
                    TT(TTm[s_][b2][i][:], B_[:, 256:384], TTm[s_][a][i][:], ALU.add, [("tb", s_, i, 2), ("TT%d" % a, s_, i)],
                       [("TT%d" % b2, s_, i)])

            for i, t in enumerate(tl):
                c0 = t * 128
                gk = gkeys(t)
                B_ = tb[i]
                gcol = gtok[:, t, hd:hd + 1].to_broadcast([128, 128])
                MM(B_[:, 0:128], kT[:, c0:c0 + 128], kT[:, c0:c0 + 128], [("kT", par)], [("tb", s_, i, 0)])
                MM(B_[:, 256:384], gcol, utri[:], gk + ["utri"], [("tb", s_, i, 2)], start=True, stop=False)
                MM(B_[:, 256:384], identb[:], mcb[:], ["identb", "mcb"], [("tb", s_, i, 2)], start=False, stop=True)
                if full:
                    MM(B_[:, 128:256], kT[:, c0:c0 + 128], qT[:, c0:c0 + 128], [("kT", par), ("qT", par)], [("tb", s_, i, 1)])
                    MM(B_[:, 384:512], gcol, utri[:], gk + ["utri"], [("tb", s_, i, 3)], start=True, stop=False)
                    MM(B_[:, 384:512], identb[:], mbb[:], ["identb", "mbb"], [("tb", s_, i, 3)], start=False, stop=True)
            for i, t in enumerate(tl):
                gk = gkeys(t)
                B_ = tb[i]
                ACT(Eo[:], B_[:, 256:384], AF.Exp, [("tb", s_, i, 2)] + gk, ["Eo"], scale=-1.0, bias=gcs[:, t, hd:hd + 1])
                STT(Cb[s_][i][:], B_[:, 0:128], nbs[:, t, hd:hd + 1], Eo[:], ALU.mult, ALU.mult,
                    [("tb", s_, i, 0), "Eo"] + gk, [("Cb", s_, i)])
                if full:
                    ACT(EiT[0][:], B_[:, 384:512], AF.Exp, [("tb", s_, i, 3)] + gk, ["EiT0"], scale=1.0,
                        bias=ngcs[:, t, hd:hd + 1])
                    TT(QKd[s_][i][:], B_[:, 128:256], EiT[0][:], ALU.mult, [("tb", s_, i, 1), "EiT0"], [("QKd", s_, i)])
            for i, t in enumerate(tl):
                pk_ = 4 * s_ + i
                TR(ptr[:, pk_ * 128:(pk_ + 1) * 128], TTm[s_][1][i][:], identb[:], [("TT1", s_, i), "identb"], [("ptr", pk_)])
                MM(tb[i][:, 0:128], Cb[s_][i][:], TTm[s_][1][i][:], [("Cb", s_, i), ("TT1", s_, i)], [("tb", s_, i, 0)])
            for i, t in enumerate(tl):
                pk_ = 4 * s_ + i
                CP(PP[s_][0][i][:, 0:128], ptr[:, pk_ * 128:(pk_ + 1) * 128], [("ptr", pk_)], [("PP0a", s_, i)])
                ACT(PP[s_][0][i][:, 128:256], tb[i][:, 0:128], AF.Copy, [("tb", s_, i, 0)], [("PP0b", s_, i)])
            for i, t in enumerate(tl):
                MM(tb[i][:, 128:256], PP[s_][0][i][:, 0:128], PP[s_][0][i][:, 128:256], [("PP0a", s_, i), ("PP0b", s_, i)],
                   [("tb", s_, i, 1)])
            for i, t in enumerate(tl):
                TT(TTm[s_][0][i][:], tb[i][:, 128:256], TTm[s_][1][i][:], ALU.add, [("tb", s_, i, 1), ("TT1", s_, i)],
                   [("TT0", s_, i)])

        def stageB(hd, b, full, par):
            kT = kTb[par]; qT = qTb[par]
            Skey = ("S", hd)
            s_ = b % 2
            tl = list(range(b * NB, b * NB + NB))
            for i, t in enumerate(tl):
                c0 = t * 128
                gk = gkeys(t)
                TTf = TTm[s_][0][i]
                MM(psn[:, 0:128], kT[:, c0:c0 + 128], Sbf[:], [("kT", par), "Sbf"], ["ps_ks"])
                if full:
                    MM(psn[:, 384:512], qT[:, c0:c0 + 128], Sbf[:], [("qT", par), "Sbf"], ["ps_qs"])
                STT(tmpb[:], psn[:, 0:128], nbegs[:, t, hd:hd + 1], vbt[s_][i][:], ALU.mult, ALU.add,
                    ["ps_ks", ("vbt", s_, i)] + gk, ["tmpb"])
                MM(psn[:, 128:256], TTf[:], tmpb[:], [("TT0", s_, i), "tmpb"], ["ps_vn"])
                ACT(vnew[:], psn[:, 128:256], AF.Copy, ["ps_vn"], ["vnew"])
                MM(psn[:, 256:384], ktl[s_][i][:], vnew[:], [("ktl", s_, i), "vnew"], ["ps_sd"])
                STT(Sbf[:], Sst[:, hd, :], gls[:, t, 0, hd:hd + 1], psn[:, 256:384], ALU.mult, ALU.add,
                    [Skey, "ps_sd"] + gk, ["Sbf"])
                STT(Sst[:, hd, :], Sst[:, hd, :], gls[:, t, 0, hd:hd + 1], psn[:, 256:384], ALU.mult, ALU.add,
                    [Skey, "ps_sd"] + gk, [Skey])
                if full:
                    MM(px[:, 128:256], QKd[s_][i][:], vnew[:], [("QKd", s_, i), "vnew"], ["px_qkv"])
                    ACT(Bc[:], px[:, 128:256], AF.Copy, ["px_qkv"], ["Bc"])
                    STT(Oh[:, t, :], psn[:, 384:512], egs[:, t, hd:hd + 1], Bc[:], ALU.mult, ALU.add,
                        ["ps_qs", "Bc"] + gk, [("Oh", t)])
                    ACT(junk[:, 0:128], Oh[:, t, :], AF.Square, [("Oh", t)], ["junk", ("ss", t)], accum_out=stat2[:, 4 + t:5 + t])

        def capture(fn, *args):
            n0 = len(P.ins)
            fn(*args)
            seg = P.ins[n0:]
            del P.ins[n0:]
            return seg

        def interleave(segB, segA):
            out = []
            ia, nA, nB_ = 0, len(segA), len(segB)
            for ib, op in enumerate(segB):
                out.append(op)
                tgt = (ib + 1) * nA // nB_
                while ia < tgt:
                    out.append(segA[ia])
                    ia += 1
            out.extend(segA[ia:])
            return out

        def gdn_head(hd, ntiles, full, par):
            szT = szTb[par]
            nb = ntiles // NB
            stageA(hd, 0, full, par)
            for b in range(1, nb):
                segA = capture(stageA, hd, b, full, par)
                segB = capture(stageB, hd, b - 1, full, par)
                P.ins.extend(interleave(segB, segA))
            stageB(hd, nb - 1, full, par)
            if full:
                ssk = [("ss", t) for t in range(ntiles)]
                TS(stat2[:, 4:12], stat2[:, 4:12], 1.0 / 128, EPS, ALU.mult, ALU.add, ssk, ["rs1"])
                ACT(stat2[:, 4:12], stat2[:, 4:12], AF.Sqrt, ["rs1"], ["rs2"])
                RECIP(stat2[:, 4:12], stat2[:, 4:12], ["rs2"], ["rs3"])
                for t in range(ntiles):
                    c0 = t * 128
                    ACT(onb[:], Oh[:, t, :], AF.Copy, [("Oh", t), "rs3"], ["onb"], scale=stat2[:, 4 + t:5 + t])
                    TR(ptr[:, 896:1024], onb[:], identb[:], ["onb", "identb"], [("ptr", 7)])
                    STT(gdnT[:, hd, c0:c0 + 128], ptr[:, 896:1024], gnw[:, 0:1], szT[:, c0:c0 + 128], ALU.mult, ALU.mult,
                        [("ptr", 7), "gnw", ("szT", par), "rs3"], [("gdnT", hd)])

        def head_load(hd, full):
            blocks = [QKV_OFF + hd * 128, QKV_OFF + 2048 + hd * 128, QKV_OFF + 4096 + hd * 128]
            if full:
                blocks.append(Z_OFF + hd * 128)
            s_ = wslot_i[0] % NWB
            wslot_i[0] += 1
            v4 = wb[s_][:, 0:KC * 512].rearrange("p (k c) -> p k c", k=KC)
            for bi, c0 in enumerate(blocks):
                P.dma("pool", "wb%d_%d" % (s_, bi), lambda h, bi=bi, c0=c0: h.dma_start(
                    out=v4[:, :, bi * 128:(bi + 1) * 128], in_=wrows("w_in", 0, KC, c0, 128)), (),
                    wkeys(s_) if bi == 0 else [("wb", s_, bi)])
            if not full:
                P.dma("pool", "wb%d_3" % s_, lambda h: h.dma_start(
                    out=v4[:, 0:1, 384:512], in_=wrows("w_in", 0, 1, Z_OFF, 128)), (), [("wb", s_, 3)])
            return v4, s_

        def head_qkv(hd, v4, s_, groups, n, hk, full, par):
            for bi, nm in enumerate(("q", "k", "v", "z")[0:4 if full else 3]):
                wk = [("wb", s_, bi)]
                if nm == "z":
                    def consz(psv, pk, g0, gn):
                        if g0 >= NT:
                            ACT(szS[:, hd, :], psv, AF.Silu, [pk], [("szS", hd)])
                        else:
                            ACT(szTb[par][:, g0:g0 + gn], psv, AF.Silu, [pk], [("szT", par)])
                    proj_chunk(v4, wk, bi * 128, groups, hk, consz)
                    continue
                chunk = {"q": 0, "k": 16, "v": 32}[nm] + hd
                if nm == "q" and not full:
                    def consq(psv, pk, g0, gn):
                        CP(pre[:, 3 + g0:3 + g0 + gn], psv, [pk], ["pre"])
                    proj_chunk(v4, wk, bi * 128, ((896, 128),), hk, consq)
                    CP(convtail[:, chunk, :], pre[:, n:n + 3], ["pre"], [("convtail", chunk)])
                    continue

                def cons(psv, pk, g0, gn, chunk=chunk):
                    if g0 >= NT:
                        CP(preS[:, chunk, :], psv, [pk], [("preS", chunk)])
                    elif g0 == 0:
                        ACT(pre[:, 3 + g0:3 + g0 + gn], psv, AF.Copy, [pk], ["pre"])
                    else:
                        CP(pre[:, 3 + g0:3 + g0 + gn], psv, [pk], ["pre"])
                if full:
                    CP(pre[:, 0:3], convtail[:, chunk, :], [("convtail", chunk)], ["pre"])
                else:
                    MEMSET(pre[:, 0:3], 0.0, ["pre"])
                proj_chunk(v4, wk, bi * 128, groups, hk, cons)
                CP(convtail[:, chunk, :], pre[:, n:n + 3], ["pre"], [("convtail", chunk)])
                dst, l2 = {"q": (qTb[par], 128.0 ** -0.5), "k": (kTb[par], 1.0), "v": (vTb[par], None)}[nm]
                conv_chunk(chunk, n, dst, (nm + "T", par), l2)

        def _main():
            for t in range(8):
                xt, xk = load_x_tile(dr["xp"][t * 128:(t + 1) * 128, :], 128)
                norm_tile(xt, 128, t, t * 128, [xk], 0)
            barrier()
            ckpt("p0norm")
            hk_prev = hkeys(range(8))
            for t in range(8):
                ab_tile(t, t * 128, 128)
            ckpt("p0ab")
            MEMSET(Sst[:, :, :], 0.0, [("S", hd) for hd in range(16)])
            for blk in range(2):
                wv, wk = wload(wrows("w_in", 0, KC, POOL_OFF + blk * 512, 512), KC, 512)
                for cc in range(4):
                    ch = blk * 4 + cc
                    ps_, pk = next_pm()
                    for k in range(KC):
                        MM(ps_[:, 0:128], wv[:, k, cc * 128:(cc + 1) * 128], hT[:, k, 896:1024], hkeys([7]) + wk, [pk],
                           start=(k == 0), stop=(k == KC - 1))
                    CP(pooltail[:, ch, :], ps_[:, 113:128], [pk], [("pooltail", ch)])
            ckpt("p0tail")
            def heads_pipeline(groups, hk, full):
                slots = {0: head_load(0, full)}
                head_qkv(0, slots[0][0], slots[0][1], groups, NT, hk, full, 0)
                slots[1] = head_load(1, full)
                for hd in range(16):
                    par = hd % 2

                    def gdn_all(hd=hd, par=par):
                        ACT(Sbf[:], Sst[:, hd, :], AF.Copy, [("S", hd)], ["Sbf"])
                        gdn_head(hd, 8, full, par)
                    segG = capture(gdn_all)
                    if hd + 1 < 16:
                        v4, s_ = slots[hd + 1]
                        segP = capture(head_qkv, hd + 1, v4, s_, groups, NT, hk, full, 1 - par)
                        P.ins.extend(interleave(segG, segP))
                        if hd + 2 < 16:
                            slots[hd + 2] = head_load(hd + 2, full)
                    else:
                        P.ins.extend(segG)
                    if full:
                        ST("o_ssm", dr["ssm_o"][hd], Sst[:, hd, :], [("S", hd)])
            heads_pipeline(TGP, hk_prev, False)
            ckpt("p0")
            dump("S0", Sst[:, :, :], [("S", hd) for hd in range(16)])
            barrier()

            for t in range(8):
                xt, xk = load_x_tile(dr["xo"][t * 128:(t + 1) * 128, :], 128)
                norm_tile(xt, 128, t, t * 128, [xk], 0)
            xt, xk = load_x_tile(dr["xs"], NS)
            norm_tile(xt, NS, 8, NT, [xk], 0)
            barrier()
            hk_own = hkeys(range(9))
            dump("hT", hT[:, :, :], hk_own)
            for t in range(8):
                ab_tile(t, t * 128, 128)
            ab_tile(8, NT, NS)
            dump("gtok", gtok[:, :, :], [("gtok", t) for t in range(9)])
            dump("btok", btok[:, :, :], [("btok", t) for t in range(9)])
            dump("gcs", gcs[:, :, :], [("gcs", t) for t in range(8)])

            ckpt("p1ab")
            ST("o_psr", dr["pool_s_rows"], dr["st_pool"][:, 1:15, :], ())
            L = NT + 15
            MEMSET(s1[:, :], 0.0, ["s1"])
            MEMSET(s2[:, :], 0.0, ["s2"])
            for gi in range(4):
                w = WINS[gi]
                wv, wk = wload(wrows("w_in", 0, KC, POOL_OFF + gi * 256, 256), KC, 256)
                for cc in range(2):
                    ch = gi * 2 + cc
                    CP(Uext[:, 0:15], pooltail[:, ch, :], [("pooltail", ch)], ["Uext"])
                    for hf in range(2):
                        LD("stp", stp[0:120, hf, :], dr["st_pool"][hf * 8:(hf + 1) * 8, :, ch * 128:(ch + 1) * 128].rearrange(
                            "i r c -> (i r) c"), ["stp"])

                    def consp(psv, pk, g0, gn, ch=ch):
                        if g0 >= NT:
                            CP(upool_s[:, ch, :], psv, [pk], [("upool_s", ch)])
                        else:
                            ACT(Uext[:, 15 + g0:15 + g0 + gn], psv, AF.Copy, [pk], ["Uext"])
                    proj_chunk(wv, wk, cc * 128, TG, hk_own, consp)
                    src = Uext
                    bufs = [s1, s2]
                    sh = 1
                    bi = 0
                    while sh < w:
                        dstb = bufs[bi % 2]
                        TT(dstb[:, sh:L], src[:, sh:L], src[:, 0:L - sh], ALU.add, ["Uext", "s1", "s2"], ["s%d" % (bi % 2 + 1)])
                        src = dstb
                        sh *= 2
                        bi += 1
                    STT(DT[:, cc, 0:NT], src[:, 15:15 + NT], 1.0 / w, Uext[:, 15:15 + NT], ALU.mult, ALU.subtract,
                        ["s1", "s2", "Uext"], [("DT", cc)])
                    TT(stat2[:, 0:16], src[:, 15:31], invcnt[:, gi * 16:(gi + 1) * 16], ALU.mult, ["s1", "s2", "invcnt"], ["pfix"])
                    TT(DT[:, cc, 0:16], stat2[:, 0:16], Uext[:, 15:31], ALU.subtract, ["pfix", "Uext", ("DT", cc)], [("DT", cc)])
                    ST("o_pt%d" % ch, dr["poolT_o"][ch * 128:(ch + 1) * 128, :], Uext[:, NT:NT + 15], ["Uext"])
                    for hf in range(2):
                        MM(px[:, 64:80], stp[0:120, hf, :], selp[:, (gi * 2 + hf) * 16:(gi * 2 + hf) * 16 + 16],
                           ["stp", "selp"], ["px_sp"], start=(hf == 0), stop=(hf == 1))
                    TT(sm16[:, 0, 0:16], px[:, 64:80], upool_s[:, ch, :], ALU.add, ["px_sp", ("upool_s", ch)], ["sm0"])
                    STT(DT[:, cc, NT:NTOK], sm16[:, 0, 0:16], 1.0 / w, upool_s[:, ch, :], ALU.mult, ALU.subtract,
                        ["sm0", ("upool_s", ch)], [("DT", cc)])
                    ST("o_psl", dr["pool_s_lastT"][ch * 128:(ch + 1) * 128, :], upool_s[:, ch, :], [("upool_s", ch)])
                if gi == 0:
                    dump("DT0", DT[:, :, :], [("DT", 0), ("DT", 1)])
                wv, wk = wload(dr["pool_w"][0, gi].rearrange("(k p) c -> p k c", p=128), 2, 256)
                for oc in range(2):
                    ch = gi * 2 + oc
                    for (g0, gn) in TG:
                        ps_, pk = next_pm()
                        for k in range(2):
                            MM(ps_[:, 0:gn], wv[:, k, oc * 128:(oc + 1) * 128], DT[:, k, g0:g0 + gn],
                               wk + [("DT", 0), ("DT", 1)], [pk], start=(k == 0), stop=(k == 1))
                        ACT(PO[:, ch, g0:g0 + gn], ps_[:, 0:gn], AF.Copy, [pk, "pscale"], [("PO", ch)], scale=pscale[:, ch:ch + 1])
            dump("PO", PO[:, :, :], [("PO", c) for c in range(8)])
            ckpt("pool")
            pok = [("PO", c) for c in range(8)]
            for jb in range(8):
                wg, wgk, wu, wuk = wload2(wrows("w_in", 0, KC, GP_OFF + jb * 256, 256), KC, 256,
                                          wrows("w_pool_up", 0, 8, jb * 256, 256), 8, 256)
                for jj in range(2):
                    j = jb * 2 + jj
                    for (g0, gn) in TG:
                        ps_, pk = next_pm()
                        for k in range(KC):
                            MM(ps_[:, 0:gn], wg[:, k, jj * 128:(jj + 1) * 128], hT[:, k, g0:g0 + gn], hk_own + wgk, [pk],
                               start=(k == 0), stop=(k == KC - 1))
                        ACT(junk[:, 0:gn], ps_[:, 0:gn], AF.Sigmoid, [pk], ["gsig"])
                        ps2, pk2 = next_pm()
                        for k in range(8):
                            MM(ps2[:, 0:gn], wu[:, k, jj * 128:(jj + 1) * 128], PO[:, k, g0:g0 + gn], pok + wuk, [pk2],
                               start=(k == 0), stop=(k == 7))
                        TT(bufM[:, j, g0:g0 + gn], ps2[:, 0:gn], junk[:, 0:gn], ALU.mult, [pk2, "gsig"], [("bufM", j)])
            dump("GA", bufM[:, :, :], [("bufM", j) for j in range(16)])
            barrier()

            ckpt("GA")
            heads_pipeline(TG, hk_own, True)
            for ch in range(48):
                ST("o_ct", dr["convT_o"][ch * 128:(ch + 1) * 128, :], convtail[:, ch, :], [("convtail", ch)])
            barrier()

            ckpt("heads1")
            def sample_path():
                LD("c11", seli[0:16, :], dr["c_seli"], ["seli"])
                ST("o_csr", dr["conv_s_rows"], dr["st_conv"][:, 1:3, :], ())
                for part in range(12):
                    LD("stc", stc[0:48, :], dr["st_conv"][:, :, part * 512:(part + 1) * 512].rearrange("i j c -> (i j) c"), ["stc"])
                    for i in range(NS):
                        LD("wrep%d" % i, wrep[i * 3:(i + 1) * 3, :], dr["conv_w"][0, 0:3, part * 512:(part + 1) * 512], [("wrep", i)])
                    TT(stc[0:48, :], stc[0:48, :], wrep[0:48, :], ALU.mult, ["stc"] + [("wrep", i) for i in range(NS)], ["stc"])
                    for h4 in range(4):
                        chunk = part * 4 + h4
                        which, hd = chunk // 16, chunk % 16
                        MM(px[:, 80:96], stc[0:48, h4 * 128:(h4 + 1) * 128], selc[:, :], ["stc", "selc"], ["px_sc"])
                        STT(sm16[:, 1, 0:16], preS[:, chunk, :], cw[:, chunk, 3:4], px[:, 80:96], ALU.mult, ALU.add,
                            ["px_sc", ("preS", chunk), "cw"], ["sm1"])
                        ST("o_csl", dr["conv_s_lastT"][chunk * 128:(chunk + 1) * 128, :], preS[:, chunk, :], [("preS", chunk)])
                        if which == 2:
                            ACT(vS[:, :, hd], sm16[:, 1, 0:16], AF.Silu, ["sm1"], ["vS"])
                        else:
                            ACT(sm16[:, 2, 0:16], sm16[:, 1, 0:16], AF.Silu, ["sm1"], ["sm2"])
                            ACT(sm16[:, 3, 0:16], sm16[:, 2, 0:16], AF.Square, ["sm2"], ["sm3"])
                            MM(px[:, 96:112], onesf[:], sm16[:, 3, 0:16], ["onesf", "sm3"], ["px_ss"])
                            ACT(sm16[:, 4, 0:16], px[:, 96:112], AF.Sqrt, ["px_ss"], ["sm4"], bias=EPS)
                            RECIP(sm16[:, 4, 0:16], sm16[:, 4, 0:16], ["sm4"], ["sm4"])
                            sc = 128.0 ** -0.5 if which == 0 else 1.0
                            STT(qkS[:, hd, :, 1 - which], sm16[:, 2, 0:16], sc, sm16[:, 4, 0:16], ALU.mult, ALU.mult,
                                ["sm2", "sm4"], ["qkS"])
                dump("qkS", qkS[:, :, :, :], ["qkS"])
                dump("vS", vS[:, :, :], ["vS"])
                for idx, src in enumerate((btok, egs)):
                    TT(xd[0:16, :, :], seli[0:16, :].rearrange("a (i e) -> a i e", i=16),
                       src[0:16, 8, :].unsqueeze(1).to_broadcast([16, 16, 16]), ALU.mult, ["seli", ("btok", 8), ("egs", 8)], ["xd"])
                    MM(px[:, 256:512], onesf[0:16, :], xd[0:16, :, :].rearrange("a i h -> a (i h)"), ["onesf", "xd"], ["px_bc"])
                    CP(bcS[:, idx, :], px[:, 256:512], ["px_bc"], [("bcS", idx)])
                TT(smpW[:, 3, :].rearrange("p (h i) -> p h i", h=16), qkS[:, :, :, 0], qkS[:, :, :, 1], ALU.mult, ["qkS"], ["smpW3"])
                MM(px[:, 256:512], onesf[:], smpW[:, 3, :], ["onesf", "smpW3"], ["px_bc"])
                CP(bcS[:, 2, :].rearrange("p (i h) -> p h i", i=16), px[:, 256:512].rearrange("p (h i) -> p h i", h=16),
                   ["px_bc"], [("bcS", 2)])
                bk = [("bcS", 0), ("bcS", 1), ("bcS", 2)]
                def ld_state(i):
                    LD("ssm%d" % (i % 2), Ssm[i % 2][:, :, :], dr["st_ssm"][i].rearrange("h d e -> d h e"),
                       [("Ssm", i % 2, hd) for hd in range(16)])
                ld_state(0)
                for i in range(NS):
                    Sb = Ssm[i % 2]
                    sk = ("Ssm", i % 2)
                    skh = [(sk[0], sk[1], hd) for hd in range(16)]
                    if i + 1 < NS:
                        ld_state(i + 1)
                    for hd in range(16):
                        MM(px[:, 256 + hd * 2:258 + hd * 2], Sb[:, hd, :], qkS[:, hd, i, :], [skh[hd], "qkS"], ["px_bc"])
                    pv = px[:, 256:288].rearrange("p (h two) -> p h two", two=2)
                    be = bcS[:, 0, i * 16:(i + 1) * 16]
                    eg_ = bcS[:, 1, i * 16:(i + 1) * 16]
                    qk_ = bcS[:, 2, i * 16:(i + 1) * 16]
                    TT(sm16[:, 6, 0:16], pv[:, :, 0], eg_, ALU.mult, ["px_bc"] + bk, ["sm6"])
                    TT(sm16[:, 6, 0:16], vS[:, i, :], sm16[:, 6, 0:16], ALU.subtract, ["vS", "sm6"], ["sm6"])
                    TT(sm16[:, 6, 0:16], sm16[:, 6, 0:16], be, ALU.mult, ["sm6"] + bk, ["sm6"])
                    TT(sm16[:, 7, 0:16], pv[:, :, 1], eg_, ALU.mult, ["px_bc"] + bk, ["sm7"])
                    TT(sm16[:, 7, 16:32], sm16[:, 6, 0:16], qk_, ALU.mult, ["sm6"] + bk, ["sm7b"])
                    TT(oS[:, i, :], sm16[:, 7, 0:16], sm16[:, 7, 16:32], ALU.add, ["sm7", "sm7b"], ["oS"])
                    for hd in range(16):
                        vcol = sm16[:, 6, hd:hd + 1].to_broadcast([128, 128])
                        pb_ = (pq, pg, psn)[hd % 3]
                        st_ = Stm2[hd % 2]
                        MM(pb_[:, 384:512], vcol, ident[:], ["sm6", "ident"], [("vbb", hd % 3)])
                        ACT(st_[:], Sb[:, hd, :], AF.Copy, [skh[hd]] + bk, [("Stm", hd % 2)],
                            scale=bcS[:, 1, i * 16 + hd:i * 16 + hd + 1])
                        STT(Sb[:, hd, :], pb_[:, 384:512], qkS[:, hd, i, 0:1], st_[:], ALU.mult, ALU.add,
                            [("vbb", hd % 3), ("Stm", hd % 2), "qkS"], [skh[hd]])
                    ST("o_ssms%d" % (i % 2), dr["ssm_s"][i].rearrange("h d e -> d h e"), Sb[:, :, :], skh)
                dump("oS", oS[:, :, :], ["oS"])
                ACT(smpW[:, 0, :], oS[:, :, :].rearrange("p i h -> p (i h)"), AF.Square, ["oS"], ["smpW0"])
                MM(px[:, 256:512], onesf[:], smpW[:, 0, :], ["onesf", "smpW0"], ["px_bc"])
                TS(smpW[:, 1, :], px[:, 256:512], 1.0 / 128, EPS, ALU.mult, ALU.add, ["px_bc"], ["smpW1"])
                ACT(smpW[:, 1, :], smpW[:, 1, :], AF.Sqrt, ["smpW1"], ["smpW1"])
                RECIP(smpW[:, 1, :], smpW[:, 1, :], ["smpW1"], ["smpW1"])
                STT(smpW[:, 2, :], oS[:, :, :].rearrange("p i h -> p (i h)"), gnw[:, 0:1], smpW[:, 1, :], ALU.mult, ALU.mult,
                    ["oS", "smpW1", "gnw"], ["smpW2"])
                sgate = smpW[:, 2, :].rearrange("p (i h) -> p i h", i=16)
                for hd in range(16):
                    TT(gdnT[:, hd, NT:NTOK], sgate[:, :, hd], szS[:, hd, :], ALU.mult, ["smpW2", ("szS", hd)], [("gdnTs", hd)])
                dump("gdnT", gdnT[:, :, :], [("gdnT", hd) for hd in range(16)] + [("gdnTs", hd) for hd in range(16)])


            def merge_pass(groups, gkn):
                gk_all = [(gkn, hd) for hd in range(16)]
                for jb in range(8):
                    wg, wgk, wu, wuk = wload2(wrows("w_in", 0, KC, GG_OFF + jb * 256, 256), KC, 256,
                                              wrows("w_gdn_up", 0, KC, jb * 256, 256), KC, 256)
                    for jj in range(2):
                        j = jb * 2 + jj
                        for (g0, gn) in groups:
                            ps_, pk = next_pm()
                            for k in range(KC):
                                MM(ps_[:, 0:gn], wg[:, k, jj * 128:(jj + 1) * 128], hT[:, k, g0:g0 + gn], hk_own + wgk, [pk],
                                   start=(k == 0), stop=(k == KC - 1))
                            ACT(junk[:, 0:gn], ps_[:, 0:gn], AF.Sigmoid, [pk], ["gsig"])
                            ps2, pk2 = next_pm()
                            for k in range(KC):
                                MM(ps2[:, 0:gn], wu[:, k, jj * 128:(jj + 1) * 128], gdnT[:, k, g0:g0 + gn], gk_all + wuk, [pk2],
                                   start=(k == 0), stop=(k == KC - 1))
                            TT(junk[:, 512:512 + gn], ps2[:, 0:gn], junk[:, 0:gn], ALU.mult, [pk2, "gsig"], ["gmul"])
                            TT(bufM[:, j, g0:g0 + gn], bufM[:, j, g0:g0 + gn], junk[:, 512:512 + gn], ALU.add,
                               ["gmul", ("bufM", j)], [("bufM", j)])


            segS = capture(sample_path)
            segM = capture(merge_pass, TG[0:2], "gdnT")
            P.ins.extend(interleave(segS, segM))
            ckpt("sample")
            merge_pass(TG[2:3], "gdnTs")
            dump("merged", bufM[:, :, :], [("bufM", j) for j in range(16)])
            ckpt("merge")
            all_keys = set()
            for I in P.ins:
                all_keys.update(I.reads)
                all_keys.update(I.writes)
            all_keys = list(all_keys) + list(UNION_KEYS)
            mk = [("bufM", j) for j in range(16)]
            tiles = [(t, 128, dr["xo"][t * 128:(t + 1) * 128, :], dr["y_o"][t * 128:(t + 1) * 128, :], t * 128) for t in range(8)]
            tiles.append((8, NS, dr["xs"], dr["y_s"], NT))
            for (t, npart, src, _, _) in tiles:
                LD("xr%d" % t, xres[0:npart, t, :], src, [("xres", t)] + (all_keys if t == 0 else []))
            for cg in range(4):
                wv, wk = wload(wrows("w_o", 0, KC, cg * 512, 512), KC, 512)
                for (t, npart, _, _, tc0) in tiles:
                    ps_, pk = next_pm()
                    for k in range(KC):
                        MM(ps_[0:npart, :], bufM[:, k, tc0:tc0 + npart], wv[:, k, :], mk + wk, [pk], start=(k == 0), stop=(k == KC - 1))
                    TT(xres[0:npart, t, cg * 512:(cg + 1) * 512], xres[0:npart, t, cg * 512:(cg + 1) * 512], ps_[0:npart, :], ALU.add,
                       [pk, ("xres", t)], [("xres", t)])
            dump("x1", xres[:, 0:8, :], [("xres", t) for t in range(8)])
            for (t, npart, _, _, tc0) in tiles:
                norm_tile(xres[0:npart, t, :], npart, t, tc0, [("xres", t)], 1)
            ckpt("wo")
            for kg in range(4):
                cbase = kg * 1408
                done = 0
                for nblk in (256, 256, 256, 256, 256, 128):
                    wg, wgk, wu, wuk = wload2(wrows("w_gate_up", 0, KC, cbase + done, nblk), KC, nblk,
                                              wrows("w_gate_up", 0, KC, DFF + cbase + done, nblk), KC, nblk)
                    for cc in range(nblk // 128):
                        a = (done // 128) + cc
                        for (g0, gn) in TG:
                            ps_, pk = next_pm()
                            for k in range(KC):
                                MM(ps_[:, 0:gn], wg[:, k, cc * 128:(cc + 1) * 128], hT[:, k, g0:g0 + gn], hk_own + wgk, [pk],
                                   start=(k == 0), stop=(k == KC - 1))
                            ACT(xn[:, 0:gn], ps_[:, 0:gn], AF.Silu, [pk], ["xn"])
                            ps2, pk2 = next_pm()
                            for k in range(KC):
                                MM(ps2[:, 0:gn], wu[:, k, cc * 128:(cc + 1) * 128], hT[:, k, g0:g0 + gn], hk_own + wuk, [pk2],
                                   start=(k == 0), stop=(k == KC - 1))
                            TT(bufM[:, a, g0:g0 + gn], ps2[:, 0:gn], xn[:, 0:gn], ALU.mult, [pk2, "xn"], [("bufM", a)])
                    done += nblk
                ak = [("bufM", a) for a in range(11)]
                for cg in range(4):
                    wv, wk = wload(wrows("w_down", cbase, 11, cg * 512, 512), 11, 512)
                    for (t, npart, _, _, tc0) in tiles:
                        ps_, pk = next_pm()
                        for k in range(11):
                            MM(ps_[0:npart, :], bufM[:, k, tc0:tc0 + npart], wv[:, k, :], ak + wk, [pk], start=(k == 0), stop=(k == 10))
                        TT(xres[0:npart, t, cg * 512:(cg + 1) * 512], xres[0:npart, t, cg * 512:(cg + 1) * 512], ps_[0:npart, :],
                           ALU.add, [pk, ("xres", t)], [("xres", t)])
            dump("x2", xres[:, 0:8, :], [("xres", t) for t in range(8)])
            ckpt("ffn")
            for (t, npart, _, _, tc0) in tiles:
                CP(xn[0:npart, :], xres[0:npart, t, :], [("xres", t)], ["xn"])
                transposes_to_hT(npart, t, tc0, None)
            ptk = [("bufM", 12), ("bufM", 13)]
            for (t, npart, _, _, tc0) in tiles:
                psrc = dr["po"][t * 128:(t + 1) * 128, :] if t < 8 else dr["psm"]
                LD("pin", pin[0:npart, :], psrc, ["pin"])
                CP(xn[0:npart, 0:256], pin[0:npart, :], ["pin"], ["xn"])
                for kk in range(2):
                    TR(ptr[:, kk * 128:kk * 128 + npart], xn[0:npart, kk * 128:(kk + 1) * 128], identb[0:npart, 0:npart],
                       ["xn", "identb"], [("ptr", kk)])
                ACT(pT[:, :, tc0:tc0 + npart], ptr[:, 0:256].rearrange("p (k t) -> p k t", k=2)[:, :, 0:npart], AF.Copy,
                    [("ptr", 0), ("ptr", 1)] + ptk, ptk)
            for cg in range(8):
                wgv, wgk, wpv, wpk = wload2(wrows("w_ple_gate", 0, KC, cg * 256, 256), KC, 256,
                                            wrows("w_ple", 0, 2, cg * 256, 256), 2, 256)
                for (t, npart, _, _, tc0) in tiles:
                    ps_, pk = next_pm()
                    for k in range(KC):
                        MM(ps_[0:npart, 0:256], hT[:, k, tc0:tc0 + npart], wgv[:, k, :], hk_own + wgk, [pk], start=(k == 0), stop=(k == KC - 1))
                    ACT(xn[0:npart, 0:256], ps_[0:npart, 0:256], AF.Sigmoid, [pk], ["xn"])
                    ps2, pk2 = next_pm()
                    for k in range(2):
                        MM(ps2[0:npart, 0:256], pT[:, k, tc0:tc0 + npart], wpv[:, k, :], ptk + wpk, [pk2], start=(k == 0), stop=(k == 1))
                    TT(xn[0:npart, 512:768], ps2[0:npart, 0:256], xn[0:npart, 0:256], ALU.mult, [pk2, "xn"], ["xnb"])
                    TT(xres[0:npart, t, cg * 256:(cg + 1) * 256], xres[0:npart, t, cg * 256:(cg + 1) * 256], xn[0:npart, 512:768],
                       ALU.add, ["xnb", ("xres", t)], [("xres", t)])
            ckpt("ple")
            nfrow = hT[:, :, :].rearrange("p k t -> p (k t)")[:, 0:4096].bitcast(F32)
            LD("nf", nfrow, dr["norm_final"].rearrange("(o d) -> o d", o=1).partition_broadcast(128), hk_own)
            for (t, npart, _, ydst, tc0) in tiles:
                xt = xres[0:npart, t, :]
                rstd_of(xt, npart, [("xres", t)])
                STT(xt, xt, stat[0:npart, 3:4], nfrow[0:npart, :], ALU.mult, ALU.mult, [("xres", t), "stat3"] + hk_own, [("xres", t)])
                ST("o_y", ydst, xt, [("xres", t)])

        try:
            _main()
        except _Stop:
            pass
        P.emit()
    return nc, dbg_out


def _consts():
    c = {}
    c["c_ident"] = np.eye(128, dtype=np.float32)
    r = np.arange(128)
    blk = (r[:, None] // 64) == (r[None, :] // 64)
    c["c_utri"] = (r[:, None] <= r[None, :]).astype(np.float32)
    c["c_ma"] = np.where((r[None, :] >= r[:, None]) | ~blk, BIG, 0.0).astype(np.float32)
    c["c_mb"] = np.where(r[None, :] < r[:, None], -BIG, 0.0).astype(np.float32)
    hm = np.zeros((128, 256), np.float32)
    hm[:, 0:128] = np.where((r[:, None] >= 64) & (r[None, :] < 64), 0.0, BIG)
    c["c_hm"] = hm
    c["c_bo"] = blk.astype(np.float32)
    selp = np.zeros((120, 4, 2, 16), np.float32)
    for gi, w in enumerate(WINS):
        for hf in range(2):
            for il in range(8):
                for rr in range(15):
                    if rr >= 15 - (w - 1):
                        selp[il * 15 + rr, gi, hf, hf * 8 + il] = 1.0
    c["c_selp"] = selp.reshape(120, 128)
    selc = np.zeros((48, 16), np.float32)
    for i in range(16):
        selc[i * 3:(i + 1) * 3, i] = 1.0
    c["c_selc"] = selc
    seli = np.zeros((16, 16, 128), np.float32)
    for i in range(16):
        seli[i, i, :] = 1.0
    c["c_seli"] = np.ascontiguousarray(seli[:, :, 0:16]).reshape(16, 256)
    return c


def _invcnt(start_pos):
    t = np.zeros((4, 16), np.float32)
    for gi, w in enumerate(WINS):
        for p in range(16):
            t[gi, p] = 1.0 / min(start_pos + p + 1, w)
    return t.reshape(1, 64)


_CACHE = {}


def make_in_maps(inp):
    consts = _consts()
    in_maps = []
    for c in range(8):
        b, hf = c // 2, c % 2
        m = {k: inp[k] for k in W_SHAPES}
        m.update(consts)
        m["xo"] = inp["x_prompt"][b, hf * NT:(hf + 1) * NT]
        m["xp"] = inp["x_prompt"][b, 0:NT] if hf == 1 else np.zeros((NT, D), np.float32)
        m["xs"] = inp["x_sample"][c * NS:(c + 1) * NS, 0]
        m["po"] = inp["p_prompt"][0, b, hf * NT:(hf + 1) * NT]
        m["psm"] = inp["p_sample"][0, c * NS:(c + 1) * NS, 0]
        m["st_pool"] = inp["state_pool"][0, c * NS:(c + 1) * NS]
        m["st_conv"] = inp["state_conv"][0, c * NS:(c + 1) * NS]
        m["st_ssm"] = inp["state_ssm"][0, c * NS:(c + 1) * NS]
        m["c_invcnt"] = _invcnt(hf * NT)
        m["l_cw"] = inp["conv_w"][0].reshape(4, 48, 128).transpose(2, 1, 0).reshape(128, 192)
        m["l_pscale"] = inp["pool_scale"][0].reshape(8, 128).T
        m["l_gnw"] = inp["gdn_norm"][0].reshape(128, 1)
        m["l_nw"] = np.concatenate([inp["norm_mix"][0].reshape(16, 128).T, inp["norm_ffn"][0].reshape(16, 128).T], axis=1)
        in_maps.append({k: np.ascontiguousarray(v) for k, v in m.items()})
    return in_maps


def kernel(**inputs):
    inp = {k: np.ascontiguousarray(np.asarray(v, dtype=np.float32)) for k, v in inputs.items()}
    if "nc" not in _CACHE:
        _CACHE["nc"] = build()[0]
    nc = _CACHE["nc"]
    in_maps = make_in_maps(inp)
    res = run_bass_kernel_spmd(nc, in_maps, core_ids=list(range(8))).results
    y_p = np.zeros((4, 2048, D), np.float32)
    y_s = np.zeros((128, 1, D), np.float32)
    pool_p = np.zeros((1, 4, 15, 1024), np.float32)
    conv_p = np.zeros((1, 4, 3, 6144), np.float32)
    ssm_p = np.zeros((1, 4, 16, 128, 128), np.float32)
    pool_s = np.zeros((1, 128, 15, 1024), np.float32)
    conv_s = np.zeros((1, 128, 3, 6144), np.float32)
    ssm_s = np.zeros((1, 128, 16, 128, 128), np.float32)
    for c in range(8):
        b, hf = c // 2, c % 2
        r = res[c]
        y_p[b, hf * NT:(hf + 1) * NT] = r["y_o"]
        y_s[c * NS:(c + 1) * NS, 0] = r["y_s"]
        if hf == 1:
            pool_p[0, b] = r["poolT_o"].T
            conv_p[0, b] = r["convT_o"].T
            ssm_p[0, b] = r["ssm_o"]
        sl = slice(c * NS, (c + 1) * NS)
        pool_s[0, sl, 0:14] = r["pool_s_rows"]
        pool_s[0, sl, 14] = r["pool_s_lastT"].T
        conv_s[0, sl, 0:2] = r["conv_s_rows"]
        conv_s[0, sl, 2] = r["conv_s_lastT"].T
        ssm_s[0, sl] = r["ssm_s"]
    return (y_p, y_s, pool_p, conv_p, ssm_p, pool_s, conv_s, ssm_s)
```

```python
import numpy as np
from contextlib import ExitStack
import concourse.bass as bass
import concourse.mybir as mybir
from concourse.bass_utils import run_bass_kernel_spmd

F32 = mybir.dt.float32
BF16 = mybir.dt.bfloat16
AF = mybir.ActivationFunctionType
ALU = mybir.AluOpType

ENGS = ("pe", "act", "dve", "pool", "sp")

D = 2048
NT = 1024
NS = 16
NTOK = NT + NS
KC = 16
POOL_OFF, QKV_OFF, Z_OFF, A_OFF, GP_OFF, GG_OFF = 0, 1024, 7168, 9216, 9248, 11296
IN_COLS = 13344
DFF = 5632
EPS = 1e-6
BIG = 32768.0
WINS = (2, 4, 8, 16)
TG = ((0, 512), (512, 512), (1024, 16))
TGP = ((0, 512), (512, 512))


class Ins:
    __slots__ = ("eng", "fn", "reads", "writes", "dsem", "needs_inc", "deps", "sig", "idx")

    def __init__(self, eng, fn, reads, writes, dsem):
        self.eng = eng
        self.fn = fn
        self.reads = tuple(reads)
        self.writes = tuple(writes)
        self.dsem = dsem
        self.needs_inc = False
        self.deps = []
        self.sig = None


class Prog:
    def __init__(self, nc):
        self.nc = nc
        self.ins = []

    def add(self, eng, fn, reads=(), writes=(), dsem=None):
        self.ins.append(Ins(eng, fn, reads, writes, dsem))

    def pe(self, fn, r=(), w=()):
        self.add("pe", fn, r, w)

    def act(self, fn, r=(), w=()):
        self.add("act", fn, r, w)

    def dve(self, fn, r=(), w=()):
        self.add("dve", fn, r, w)

    def pool(self, fn, r=(), w=()):
        self.add("pool", fn, r, w)

    def dma(self, eng, dsem, fn, r=(), w=()):
        self.add(eng, fn, r, w, dsem=dsem)

    def analyze(self):
        last_w, rd_eng, rd_dma = {}, {}, {}
        for i, I in enumerate(self.ins):
            I.idx = i
            deps = {}
            for k in I.reads:
                d = last_w.get(k)
                if d is not None:
                    deps[d] = "raw"
            for k in I.writes:
                d = last_w.get(k)
                if d is not None and d not in deps:
                    deps[d] = "waw"
                for d in rd_eng.get(k, {}).values():
                    if d not in deps:
                        deps[d] = "war"
                for d in rd_dma.get(k, ()):
                    if d not in deps:
                        deps[d] = "war"
            for d, kind in deps.items():
                Dn = self.ins[d]
                if Dn.dsem is None and Dn.eng == I.eng and I.dsem is None:
                    if I.eng == "pe" or kind == "war":
                        continue
                I.deps.append(d)
                if Dn.dsem is None:
                    Dn.needs_inc = True
            for k in I.reads:
                if I.dsem is not None:
                    rd_dma.setdefault(k, []).append(i)
                else:
                    rd_eng.setdefault(k, {})[I.eng] = i
            for k in I.writes:
                last_w[k] = i
                rd_eng[k] = {}
                rd_dma[k] = []
        cnt = {e: 0 for e in ENGS}
        dcnt = {}
        for I in self.ins:
            if I.dsem is not None:
                dcnt[I.dsem] = dcnt.get(I.dsem, 0) + 16
                I.sig = (("d", I.dsem), dcnt[I.dsem])
            elif I.needs_inc:
                cnt[I.eng] += 1
                I.sig = (("e", I.eng), cnt[I.eng])
        self.final_counts = (cnt, dcnt)

    def emit(self, final_eng="sp"):
        nc = self.nc
        self.analyze()
        with ExitStack() as es:
            esem = {e: es.enter_context(nc.semaphore("c_" + e)) for e in ENGS}
            dnames = sorted({I.dsem for I in self.ins if I.dsem is not None})
            dsem = {n: es.enter_context(nc.semaphore("d_" + n)) for n in dnames}
            block = es.enter_context(nc.Block())

            def semof(key):
                return esem[key[1]] if key[0] == "e" else dsem[key[1]]

            def run_engine(ename, h):
                known = {}
                for I in self.ins:
                    if I.eng != ename:
                        continue
                    need = {}
                    for d in I.deps:
                        key, val = self.ins[d].sig
                        if need.get(key, 0) < val:
                            need[key] = val
                    for key, val in need.items():
                        if known.get(key, 0) >= val:
                            continue
                        h.wait_ge(semof(key), val)
                        known[key] = val
                    bi = I.fn(h)
                    if I.dsem is not None:
                        bi.then_inc(dsem[I.dsem], 16)
                    elif I.needs_inc:
                        bi.then_inc(esem[ename], 1)
                if ename == final_eng:
                    cnt, dcnt = self.final_counts
                    for n, v in dcnt.items():
                        if known.get(("d", n), 0) < v:
                            h.wait_ge(dsem[n], v)
                    for e, v in cnt.items():
                        if v > 0 and e != ename and known.get(("e", e), 0) < v:
                            h.wait_ge(esem[e], v)

            @block.sync
            def _(h):
                run_engine("sp", h)

            @block.tensor
            def _(h):
                run_engine("pe", h)

            @block.scalar
            def _(h):
                run_engine("act", h)

            @block.vector
            def _(h):
                run_engine("dve", h)

            @block.gpsimd
            def _(h):
                run_engine("pool", h)


W_SHAPES = {
    "norm_mix": [1, D], "w_in": [1, D, IN_COLS], "pool_w": [1, 4, 256, 256], "pool_scale": [1, 1024],
    "conv_w": [1, 4, 6144], "a_log": [1, 16], "dt_bias": [1, 16], "gdn_norm": [1, 128],
    "w_pool_up": [1, 1024, D], "w_gdn_up": [1, D, D], "w_o": [1, D, D], "norm_ffn": [1, D],
    "w_gate_up": [1, D, 2 * DFF], "w_down": [1, DFF, D], "w_ple": [1, 256, D], "w_ple_gate": [1, D, D],
    "norm_final": [D],
}
IN_SHAPES = {
    "xo": [NT, D], "xp": [NT, D], "xs": [NS, D], "po": [NT, 256], "psm": [NS, 256],
    "st_pool": [NS, 15, 1024], "st_conv": [NS, 3, 6144], "st_ssm": [NS, 16, 128, 128],
    "c_ident": [128, 128], "c_utri": [128, 128], "c_hm": [128, 256], "c_bo": [128, 128], "c_ma": [128, 128], "c_mb": [128, 128],
    "c_invcnt": [1, 64], "c_selp": [120, 128], "c_selc": [48, 16], "c_seli": [16, 256],
    "l_cw": [128, 192], "l_pscale": [128, 8], "l_gnw": [128, 1], "l_nw": [128, 32],
}
OUT_SHAPES = {
    "y_o": [NT, D], "y_s": [NS, D], "poolT_o": [1024, 15], "convT_o": [6144, 3], "ssm_o": [16, 128, 128],
    "pool_s_rows": [NS, 14, 1024], "pool_s_lastT": [1024, NS], "conv_s_rows": [NS, 2, 6144],
    "conv_s_lastT": [6144, NS], "ssm_s": [NS, 16, 128, 128],
}


class _Stop(Exception):
    pass


def build(debug=(), stop=None):
    nc = bass.Bass("TRN2", target_bir_lowering=False)
    dr = {}
    for n, s in list(W_SHAPES.items()) + list(IN_SHAPES.items()):
        dr[n] = nc.dram_tensor(n, s, F32, kind="ExternalInput").ap()
    for n, s in OUT_SHAPES.items():
        dr[n] = nc.dram_tensor(n, s, F32, kind="ExternalOutput").ap()
    dbg_out = {}
    P = Prog(nc)
    es = ExitStack()
    with es:
        def sb(name, shape, dt=F32):
            return es.enter_context(nc.sbuf_tensor(name, shape, dt))

        def pst(name, shape, dt=F32):
            return es.enter_context(nc.psum_tensor(name, shape, dt))

        PSUM_NAMES = ("pm0", "pm1", "pm2", "pg", "ptr", "pq", "psn", "px")

        def bankkeys(*aps):
            ks = []
            for a_ in aps:
                n_ = getattr(a_, "name", None)
                if n_ in PSUM_NAMES and ("bank", n_) not in ks:
                    ks.append(("bank", n_))
            return ks

        def MM(out, lhsT, rhs, r, w, start=True, stop=True):
            P.pe(lambda h: h.matmul(out, lhsT=lhsT, rhs=rhs, start=start, stop=stop), r, list(w) + bankkeys(out))

        def TR(out, in_, idn, r, w):
            P.pe(lambda h: h.transpose(out, in_, idn), r, list(w) + bankkeys(out))

        def ACT(out, in_, func, r, w, **kw):
            P.act(lambda h: h.activation(out=out, in_=in_, func=func, **kw), r, list(w) + bankkeys(out, in_))

        def TS(out, in0, s1, s2, op0, op1, r, w, eng="dve"):
            w = list(w) + bankkeys(out, in0)
            if s2 is None:
                P.add(eng, lambda h: h.tensor_scalar(out=out, in0=in0, scalar1=s1, scalar2=None, op0=op0), r, w)
            else:
                P.add(eng, lambda h: h.tensor_scalar(out=out, in0=in0, scalar1=s1, scalar2=s2, op0=op0, op1=op1), r, w)

        def STT(out, in0, scalar, in1, op0, op1, r, w):
            P.dve(lambda h: h.scalar_tensor_tensor(out=out, in0=in0, scalar=scalar, in1=in1, op0=op0, op1=op1), r,
                  list(w) + bankkeys(out, in0, in1))

        def TT(out, in0, in1, op, r, w, eng="dve"):
            P.add(eng, lambda h: h.tensor_tensor(out=out, in0=in0, in1=in1, op=op), r, list(w) + bankkeys(out, in0, in1))

        def CP(out, in_, r, w, eng="dve"):
            P.add(eng, lambda h: h.tensor_copy(out=out, in_=in_), r, list(w) + bankkeys(out, in_))

        def RECIP(out, in_, r, w):
            P.dve(lambda h: h.reciprocal(out=out, in_=in_), r, w)

        def MEMSET(ap_, val, w):
            P.dve(lambda h: h.memset(ap_, val), (), w)

        def LD(dsem, out, in_, w, r=(), eng="sp"):
            P.dma(eng, dsem, lambda h: h.dma_start(out=out, in_=in_), r, w)

        def ST(dsem, out, in_, r, eng="sp"):
            P.dma(eng, dsem, lambda h: h.dma_start(out=out, in_=in_), r, ())

        def dump(name, ap_, rkeys):
            if name in debug:
                o = nc.dram_tensor("dbg_" + name, list(ap_.shape), ap_.dtype, kind="ExternalOutput").ap()
                dbg_out[name] = o
                ST("dbg_" + name, o, ap_, rkeys)

        ident = sb("ident", [128, 128]); identb = sb("identb", [128, 128], BF16)
        utri = sb("utri", [128, 128]); onesf = sb("onesf", [128, 128]); onesb = sb("onesb", [128, 128], BF16)
        mab = sb("mab", [128, 128], BF16); mbb = sb("mbb", [128, 128], BF16); mcb = sb("mcb", [128, 128], BF16)
        mstage = sb("mstage", [128, 256])
        halfm = sb("halfm", [128, 256]); blkones = sb("blkones", [128, 128])
        invcnt = sb("invcnt", [128, 64])
        nwfm = sb("nwfm", [128, 2, 16])
        cw = sb("cw", [128, 48, 4])
        pscale = sb("pscale", [128, 8])
        gnw = sb("gnw", [128, 1])
        dtb = sb("dtb", [128, 16]); nA = sb("nA", [128, 16])
        bar = sb("bar", [128, 2]); epsc = sb("epsc", [128, 1])
        hT = sb("hT", [128, KC, NTOK], BF16)
        bufM = sb("bufM", [128, 16, NTOK], BF16)
        pT = bufM[:, 12:14, :]
        NWB = 2
        wb = [sb("wb%d" % i, [128, 8192], BF16) for i in range(NWB)]
        BIGN = 20736 + 22272
        big = sb("big", [128, BIGN], BF16)
        xres = big[:, 0:36864].bitcast(F32).rearrange("p (t d) -> p t d", t=9)

        def carver(base):
            st_ = [base]

            def carve(nelem_bf16, dt=BF16):
                a_ = st_[0]
                st_[0] += nelem_bf16
                assert st_[0] <= BIGN, (st_[0], BIGN)
                v = big[:, a_:a_ + nelem_bf16]
                return v.bitcast(F32) if dt == F32 else v
            return carve
        c0_ = carver(0)
        gdnT = c0_(16 * NTOK).rearrange("p (k t) -> p k t", k=16)
        Sst = c0_(2 * 16 * 128, F32).rearrange("p (h e) -> p h e", h=16)
        UB = 20736
        cA = carver(UB)
        pre = cA(2 * (NTOK + 8), F32); acc = cA(2 * NTOK, F32)
        qTb = [cA(NTOK), cA(NTOK)]; kTb = [cA(NTOK), cA(NTOK)]; vTb = [cA(NTOK), cA(NTOK)]; szTb = [cA(NTOK), cA(NTOK)]
        rinv = cA(1024, F32); Sbf = cA(128)
        Oh = cA(2 * 8 * 128, F32).rearrange("p (t e) -> p t e", t=8)
        NB = 2
        ktl = [[cA(128) for _ in range(NB)] for _ in range(2)]; vbt = [[cA(128) for _ in range(NB)] for _ in range(2)]
        QKd = [[cA(128) for _ in range(NB)] for _ in range(2)]
        Es = [cA(256, F32)]; EiT = [cA(256, F32)]; Eo = cA(256, F32)
        Cb = [[cA(128) for _ in range(NB)] for _ in range(2)]
        PP = [[[cA(256) for _ in range(NB)] for _ in range(2)] for _ in range(2)]
        TTm = [[[cA(128) for _ in range(NB)] for _ in range(2)] for _ in range(2)]
        tmpb = cA(128); vnew = cA(128); Bc = cA(256, F32); onb = cA(128)
        cX = carver(UB)
        xin = [cX(4096, F32)]
        cP = carver(UB)
        Uext = cP(2 * (NTOK + 16), F32); s1 = cP(2 * (NTOK + 16), F32); s2 = cP(2 * (NTOK + 16), F32)
        DT = cP(2 * NTOK).rearrange("p (k t) -> p k t", k=2)
        PO = cP(8 * NTOK).rearrange("p (k t) -> p k t", k=8)
        stp = cP(2 * 2 * 128, F32).rearrange("p (h c) -> p h c", h=2)
        cS = carver(UB)
        Ssm = [cS(4096, F32).rearrange("p (h e) -> p h e", h=16) for _ in range(2)]
        qkS = cS(2 * 16 * NS * 2, F32).rearrange("p (h i t) -> p h i t", h=16, i=NS)
        vS = cS(2 * NS * 16, F32).rearrange("p (i h) -> p i h", i=NS)
        oS = cS(2 * NS * 16, F32).rearrange("p (i h) -> p i h", i=NS)
        bcS = cS(2 * 3 * 256, F32).rearrange("p (a n) -> p a n", a=3)
        smpW = cS(2 * 4 * 256, F32).rearrange("p (a n) -> p a n", a=4)
        Stm = cS(256, F32); Stm2 = [Stm, cS(256, F32)]
        stc = cS(2 * 512, F32); wrep = cS(2 * 512, F32)
        seli = cS(2 * 256, F32); xd = cS(2 * 256, F32).rearrange("p (i h) -> p i h", i=16)
        UNION_KEYS = (["pre", "acc", "xn", "junk", "gsig", "gmul", "sq", "rinv", "Sbf", "tmpb", "vnew", "Bc", "onb", "Es0", "EiT0", "Eo", ("Cb", 0, 0), ("Cb", 0, 1), ("Cb", 1, 0), ("Cb", 1, 1), ("qT", 0), ("qT", 1), ("kT", 0), ("kT", 1), ("vT", 0), ("vT", 1),
                       ("szT", 0), ("szT", 1), ("xin", 0), "Uext", "s1", "s2",
                       "stp", ("DT", 0), ("DT", 1), "qkS", "vS", "oS", "smpW0", "smpW1", "smpW2", "smpW3", ("Stm", 0), ("Stm", 1),
                       "stc", "seli", "xd"] + [("Oh", t) for t in range(8)] + [("ss", t) for t in range(8)] +
                      [(n_, s_, i_) for n_ in ("ktl", "vbt", "QKd", "PP0a", "PP0b", "PP1a", "PP1b", "TT0", "TT1") for s_ in range(2)
                       for i_ in range(2)] +
                      [("PO", c) for c in range(8)] + [("bcS", a_) for a_ in range(3)] + [("wrep", i) for i in range(NS)] + [("Ssm", a_, h_) for a_ in range(2) for h_ in range(16)])
        gtok = sb("gtok", [128, 9, 16]); btok = sb("btok", [128, 9, 16])
        gcs = sb("gcs", [128, 9, 16]); egs = sb("egs", [128, 9, 16]); ets = sb("ets", [128, 9, 16])
        gls = sb("gls", [128, 9, 2, 16]); nbs = sb("nbs", [128, 9, 16]); nbegs = sb("nbegs", [128, 9, 16])
        ngcs = sb("ngcs", [128, 9, 16])
        tmp16 = sb("tmp16", [128, 4, 16])
        convtail = sb("convtail", [128, 48, 3]); pooltail = sb("pooltail", [128, 8, 15])
        stat = sb("stat", [128, 16]); stat2 = sb("stat2", [128, 16])
        xn = sb("xn", [128, D], BF16)
        sq = xn[:, 0:512]; junk = xn[:, 1024:2048]
        pin = sb("pin", [128, 256])
        wab = sb("wab", [128, KC, 32], BF16)
        szS = sb("szS", [128, 16, NS])
        sm16 = sb("sm16", [128, 8, 32])
        selp = sb("selp", [120, 128]); selc = sb("selc", [48, 16])
        upool_s = sb("upool_s", [128, 8, NS]); preS = sb("preS", [128, 48, NS])

        def barrier():
            P.dve(lambda h: h.memset(bar[:, 0:1], 0.0), (), list(UNION_KEYS))

        def ckpt(name):
            if stop == name:
                raise _Stop()

        pm = [pst("pm%d" % i, [128, 512]) for i in range(3)]
        pg = pst("pg", [128, 512])
        ptr = pst("ptr", [128, 1024], BF16)
        pq = pst("pq", [128, 512])
        psn = pst("psn", [128, 512])
        px = pst("px", [128, 512])
        tbanks = [[pg, pq], [pg, pq]]
        pm_i = [0]

        def next_pm():
            i = pm_i[0] % 3
            pm_i[0] += 1
            return pm[i], ("pm", i)

        wslot_i = [0]

        def wkeys(s_):
            return [("wb", s_, q) for q in range(4)]

        def wload(src_ap, kc, ncols):
            s_ = wslot_i[0] % NWB
            wslot_i[0] += 1
            v = wb[s_][:, 0:kc * ncols].rearrange("p (k c) -> p k c", k=kc)
            P.dma("pool", "wb%d" % s_, lambda h: h.dma_start(out=v, in_=src_ap), (), wkeys(s_))
            return v, wkeys(s_)

        def wload2(srcA, kcA, ncA, srcB, kcB, ncB):
            s_ = wslot_i[0] % NWB
            wslot_i[0] += 1
            assert kcA * ncA <= 4096 and kcB * ncB <= 4096
            vA = wb[s_][:, 0:kcA * ncA].rearrange("p (k c) -> p k c", k=kcA)
            vB = wb[s_][:, 4096:4096 + kcB * ncB].rearrange("p (k c) -> p k c", k=kcB)
            P.dma("pool", "wb%d_0" % s_, lambda h: h.dma_start(out=vA, in_=srcA), (), wkeys(s_))
            P.dma("pool", "wb%d_1" % s_, lambda h: h.dma_start(out=vB, in_=srcB), (), [("wb", s_, 2), ("wb", s_, 3)])
            return vA, [("wb", s_, 0), ("wb", s_, 1)], vB, [("wb", s_, 2), ("wb", s_, 3)]

        def wrows(name, r0, kc, c0, ncols):
            a = dr[name][0] if len(W_SHAPES[name]) == 3 else dr[name]
            return a[r0:r0 + kc * 128, c0:c0 + ncols].rearrange("(k p) c -> p k c", p=128)

        LD("c0", ident[:], dr["c_ident"], ["ident"])
        LD("c1", utri[:], dr["c_utri"], ["utri"])
        LD("c2", mstage[:, 0:128], dr["c_ma"], ["mst0"])
        LD("c2b", mstage[:, 128:256], dr["c_mb"], ["mst1"])
        LD("c13", halfm[:], dr["c_hm"], ["halfm"])
        LD("c14", blkones[:], dr["c_bo"], ["blkones"])
        LD("c3", invcnt[:], dr["c_invcnt"].partition_broadcast(128), ["invcnt"])
        LD("c4", cw[:].rearrange("p k t -> p (k t)"), dr["l_cw"], ["cw"])
        LD("c5", pscale[:], dr["l_pscale"], ["pscale"])
        LD("c6", gnw[:], dr["l_gnw"], ["gnw"])
        LD("c7", dtb[:], dr["dt_bias"].partition_broadcast(128), ["dtb"])
        LD("c8", nA[:], dr["a_log"].partition_broadcast(128), ["nA"])
        LD("c9", selp[:], dr["c_selp"], ["selp"])
        LD("c10", selc[:], dr["c_selc"], ["selc"])
        P.dma("pool", "wab", lambda h: h.dma_start(out=wab[:], in_=wrows("w_in", 0, KC, A_OFF, 32)), (), ["wab"])
        LD("c12", nwfm[:].rearrange("p a k -> p (a k)"), dr["l_nw"], ["nwfm"])
        CP(identb[:], ident[:], ["ident"], ["identb"])
        CP(mab[:], mstage[:, 0:128], ["mst0"], ["mab"])
        CP(mbb[:], mstage[:, 128:256], ["mst1"], ["mbb"])
        CP(mcb[:], halfm[:, 0:128], ["halfm"], ["mcb"])
        MEMSET(onesf[:], 1.0, ["onesf"])
        MEMSET(onesb[:], 1.0, ["onesb"])
        MEMSET(epsc[:], EPS, ["epsc"])
        ACT(nA[:], nA[:], AF.Exp, ["nA"], ["nA"])
        TS(nA[:], nA[:], -1.0, None, ALU.mult, None, ["nA"], ["nA"])

        def transposes_to_hT(npart, t, tcol0, which):
            for k4 in range(4):
                for kk in range(4):
                    k = k4 * 4 + kk
                    TR(ptr[:, kk * 128:kk * 128 + npart], xn[0:npart, k * 128:(k + 1) * 128], identb[0:npart, 0:npart],
                       ["xn", "identb"], [("ptr", kk)])
                dst = hT[:, k4 * 4:k4 * 4 + 4, tcol0:tcol0 + npart]
                src = ptr[:, 0:512].rearrange("p (k t) -> p k t", k=4)[:, :, 0:npart]
                rk = [("ptr", kk) for kk in range(4)]
                if which is None:
                    if k4 % 2 == 0:
                        ACT(dst, src, AF.Copy, rk, [("hT", t, k4)])
                    else:
                        CP(dst, src, rk, [("hT", t, k4)])
                else:
                    TT(dst, src, nwfm[:, which, k4 * 4:k4 * 4 + 4].unsqueeze(2).to_broadcast([128, 4, npart]), ALU.mult,
                       rk + ["nwfm"], [("hT", t, k4)])

        def rstd_of(xt, npart, xkeys):
            ACT(xn[0:npart, :], xt, AF.Square, xkeys, ["xn", "stat"], accum_out=stat[0:npart, 0:1])
            TS(stat[0:npart, 1:2], stat[0:npart, 0:1], 1.0 / D, EPS, ALU.mult, ALU.add, ["stat"], ["stat1"])
            ACT(stat[0:npart, 2:3], stat[0:npart, 1:2], AF.Sqrt, ["stat1"], ["stat2"])
            RECIP(stat[0:npart, 3:4], stat[0:npart, 2:3], ["stat2"], ["stat3"])

        def norm_tile(xt, npart, t, tcol0, xkeys, which):
            rstd_of(xt, npart, xkeys)
            ACT(xn[0:npart, :], xt, AF.Copy, list(xkeys) + ["stat3"], ["xn"], scale=stat[0:npart, 3:4])
            transposes_to_hT(npart, t, tcol0, which)

        def load_x_tile(src_ap, npart):
            LD("xin0", xin[0][0:npart, :], src_ap, [("xin", 0)])
            return xin[0][0:npart, :], ("xin", 0)

        def hkeys(tl):
            return [("hT", t, k4) for t in tl for k4 in range(4)]

        def ab_tile(tile, tcol0, npart):
            for k in range(KC):
                MM(px[0:npart, 0:32], hT[:, k, tcol0:tcol0 + npart], wab[:, k, :], hkeys([tile]) + ["wab"], ["px_ab"],
                   start=(k == 0), stop=(k == KC - 1))
            ACT(tmp16[0:npart, 0, :], px[0:npart, 16:32], AF.Exp, ["px_ab"], ["t16_0"], scale=-1.0)
            TS(tmp16[0:npart, 0, :], tmp16[0:npart, 0, :], 1.0, None, ALU.add, None, ["t16_0"], ["t16_0"])
            RECIP(btok[0:npart, tile, :], tmp16[0:npart, 0, :], ["t16_0"], [("btok", tile)])
            TT(tmp16[0:npart, 1, :], px[0:npart, 0:16], dtb[0:npart, :], ALU.add, ["px_ab", "dtb"], ["t16_1"])
            ACT(tmp16[0:npart, 1, :], tmp16[0:npart, 1, :], AF.Exp, ["t16_1"], ["t16_1"])
            ACT(tmp16[0:npart, 1, :], tmp16[0:npart, 1, :], AF.Ln, ["t16_1"], ["t16_1"], bias=1.0)
            TT(gtok[0:npart, tile, :], tmp16[0:npart, 1, :], nA[0:npart, :], ALU.mult, ["t16_1", "nA"], [("gtok", tile)])
            TS(nbs[0:npart, tile, :], btok[0:npart, tile, :], -1.0, None, ALU.mult, None, [("btok", tile)], [("nbs", tile)])
            if npart == 128:
                MM(px[:, 32:48], utri[:], gtok[:, tile, :], ["utri", ("gtok", tile)], ["px_gc"])
                MM(px[:, 48:64], onesf[:], gtok[:, tile, :], ["onesf", ("gtok", tile)], ["px_gl"])
                CP(gcs[:, tile, :], px[:, 32:48], ["px_gc"], [("gcs", tile)])
                TS(ngcs[:, tile, :], px[:, 32:48], -1.0, None, ALU.mult, None, ["px_gc"], [("ngcs", tile)])
                ACT(egs[:, tile, :], px[:, 32:48], AF.Exp, ["px_gc"], [("egs", tile)])
                ACT(gls[:, tile, 0, :], px[:, 48:64], AF.Exp, ["px_gl"], [("gls", tile)])
                TT(tmp16[:, 2, :], px[:, 48:64], gcs[:, tile, :], ALU.subtract, ["px_gl", ("gcs", tile)], ["t16_2"])
                ACT(ets[:, tile, :], tmp16[:, 2, :], AF.Exp, ["t16_2"], [("ets", tile)])
                TT(nbegs[:, tile, :], nbs[:, tile, :], egs[:, tile, :], ALU.mult, [("nbs", tile), ("egs", tile)],
                   [("nbegs", tile)])
            else:
                ACT(egs[0:npart, tile, :], gtok[0:npart, tile, :], AF.Exp, [("gtok", tile)], [("egs", tile)])

        def gkeys(tile):
            return [(n, tile) for n in ("gtok", "gcs", "ngcs", "egs", "ets", "gls", "nbs", "nbegs", "btok")]

        def proj_chunk(wv, wk, wcol, groups, hk, consume):
            for (g0, gn) in groups:
                ps_, pk = next_pm()
                for k in range(KC):
                    MM(ps_[:, 0:gn], wv[:, k, wcol:wcol + 128], hT[:, k, g0:g0 + gn], hk + wk, [pk],
                       start=(k == 0), stop=(k == KC - 1))
                consume(ps_[:, 0:gn], pk, g0, gn)

        def conv_chunk(chunk, n, dst, dkey, l2scale):
            w4 = cw[:, chunk, :]
            ACT(acc[:, 0:n], pre[:, 0:n], AF.Copy, ["pre", "cw"], ["acc"], scale=w4[:, 0:1])
            for t in (1, 2, 3):
                STT(acc[:, 0:n], pre[:, t:t + n], w4[:, t:t + 1], acc[:, 0:n], ALU.mult, ALU.add, ["pre", "acc", "cw"], ["acc"])
            if l2scale is None:
                ACT(dst[:, 0:n], acc[:, 0:n], AF.Silu, ["acc"], [dkey])
                return
            ACT(acc[:, 0:n], acc[:, 0:n], AF.Silu, ["acc"], ["acc"])
            for g0 in range(0, n, 512):
                gn = min(512, n - g0)
                ACT(sq[:, 0:gn], acc[:, g0:g0 + gn], AF.Square, ["acc"], ["sq"])
                ps_, pk = next_pm()
                MM(ps_[:, 0:gn], onesb[:], sq[:, 0:gn], ["onesb", "sq"], [pk])
                ACT(rinv[:, 0:gn], ps_[:, 0:gn], AF.Ln, [pk, "epsc"], ["rinv"], bias=epsc[:, 0:1])
                ACT(rinv[:, 0:gn], rinv[:, 0:gn], AF.Exp, ["rinv"], ["rinv"], scale=-0.5)
                STT(dst[:, g0:g0 + gn], acc[:, g0:g0 + gn], l2scale, rinv[:, 0:gn], ALU.mult, ALU.mult,
                    ["acc", "rinv"], [dkey])

        def stageA(hd, b, full, par):
            kT = kTb[par]; vT = vTb[par]; qT = qTb[par]
            s_ = b % 2
            tb = tbanks[s_]
            tl = list(range(b * NB, b * NB + NB))
            for i, t in enumerate(tl):
                c0 = t * 128
                TR(ptr[:, (4 * s_ + 2 * i) * 128:(4 * s_ + 2 * i + 1) * 128], kT[:, c0:c0 + 128], identb[:], [("kT", par), "identb"],
                   [("ptr", 4 * s_ + 2 * i)])
                TR(ptr[:, (4 * s_ + 2 * i + 1) * 128:(4 * s_ + 2 * i + 2) * 128], vT[:, c0:c0 + 128], identb[:], [("vT", par), "identb"],
                   [("ptr", 4 * s_ + 2 * i + 1)])
            for i, t in enumerate(tl):
                gk = gkeys(t)
                pk_ = 4 * s_ + 2 * i
                ACT(ktl[s_][i][:], ptr[:, pk_ * 128:(pk_ + 1) * 128], AF.Copy, [("ptr", pk_)] + gk, [("ktl", s_, i)],
                    scale=ets[:, t, hd:hd + 1])
                TS(vbt[s_][i][:], ptr[:, (pk_ + 1) * 128:(pk_ + 2) * 128], btok[:, t, hd:hd + 1], None, ALU.mult, None,
                   [("ptr", pk_ + 1)] + gk, [("vbt", s_, i)])
            for i, t in enumerate(tl):
                c0 = t * 128
                gk = gkeys(t)
                B_ = tb[i]
                gcol = gtok[:, t, hd:hd + 1].to_broadcast([128, 128])
                MM(B_[:, 0:128], kT[:, c0:c0 + 128], kT[:, c0:c0 + 128], [("kT", par)], [("tb", s_, i, 0)])
                MM(B_[:, 256:384], gcol, utri[:], gk + ["utri"], [("tb", s_, i, 2)], start=True, stop=False)
                MM(B_[:, 256:384], identb[:], mab[:], ["identb", "mab"], [("tb", s_, i, 2)], start=False, stop=True)
                if full:
                    MM(B_[:, 128:256], kT[:, c0:c0 + 128], qT[:, c0:c0 + 128], [("kT", par), ("qT", par)], [("tb", s_, i, 1)])
                    MM(B_[:, 384:512], gcol, utri[:], gk + ["utri"], [("tb", s_, i, 3)], start=True, stop=False)
                    MM(B_[:, 384:512], identb[:], mbb[:], ["identb", "mbb"], [("tb", s_, i, 3)], start=False, stop=True)
            for i, t in enumerate(tl):
                gk = gkeys(t)
                B_ = tb[i]
                e_ = 0
                ACT(Es[e_][:], B_[:, 256:384], AF.Exp, [("tb", s_, i, 2)] + gk, ["Es%d" % e_], scale=-1.0, bias=gcs[:, t, hd:hd + 1])
                STT(PP[s_][0][i][:, 0:128], B_[:, 0:128], nbs[:, t, hd:hd + 1], Es[e_][:], ALU.mult, ALU.mult,
                    [("tb", s_, i, 0), "Es%d" % e_] + gk, [("PP0a", s_, i)])
                if full:
                    ACT(EiT[e_][:], B_[:, 384:512], AF.Exp, [("tb", s_, i, 3)] + gk, ["EiT%d" % e_], scale=1.0,
                        bias=ngcs[:, t, hd:hd + 1])
                    TT(QKd[s_][i][:], B_[:, 128:256], EiT[e_][:], ALU.mult, [("tb", s_, i, 1), "EiT%d" % e_], [("QKd", s_, i)])
            for i, t in enumerate(tl):
                gk = gkeys(t)
                B_ = tb[i]
                gcol = gtok[:, t, hd:hd + 1].to_broadcast([128, 128])
                MM(B_[:, 256:384], gcol, utri[:], gk + ["utri"], [("tb", s_, i, 2)], start=True, stop=False)
                MM(B_[:, 256:384], identb[:], mcb[:], ["identb", "mcb"], [("tb", s_, i, 2)], start=False, stop=True)
                ACT(Eo[:], B_[:, 256:384], AF.Exp, [("tb", s_, i, 2)] + gk, ["Eo"], scale=-1.0, bias=gcs[:, t, hd:hd + 1])
                STT(Cb[s_][i][:], B_[:, 0:128], nbs[:, t, hd:hd + 1], Eo[:], ALU.mult, ALU.mult,
                    [("tb", s_, i, 0), "Eo"] + gk, [("Cb", s_, i)])
            for i, t in enumerate(tl):
                pk_ = 4 * s_ + i
                TR(ptr[:, pk_ * 128:(pk_ + 1) * 128], PP[s_][0][i][:, 0:128], identb[:], [("PP0a", s_, i), "identb"], [("ptr", pk_)])
            for i, t in enumerate(tl):
                pk_ = 4 * s_ + i
                ACT(PP[s_][0][i][:, 128:256], ptr[:, pk_ * 128:(pk_ + 1) * 128], AF.Copy, [("ptr", pk_)], [("PP0b", s_, i)])
                TT(TTm[s_][0][i][:], ptr[:, pk_ * 128:(pk_ + 1) * 128], identb[:], ALU.add, [("ptr", pk_), "identb"], [("TT0", s_, i)])
            for j in range(1, 6):
                a, b2 = (j - 1) % 2, j % 2
                for i, t in enumerate(tl):
                    B_ = tb[i]
                    ka = [("PP%da" % a, s_, i), ("PP%db" % a, s_, i)]
                    MM(B_[:, 0:128], PP[s_][a][i][:, 128:256], PP[s_][a][i][:, 0:128], ka, [("tb", s_, i, 0)])
                    if j < 5:
                        MM(B_[:, 128:256], PP[s_][a][i][:, 0:128], PP[s_][a][i][:, 128:256], ka, [("tb", s_, i, 1)])
                for i, t in enumerate(tl):
                    B_ = tb[i]
                    n_ = 256 if j < 5 else 128
                    wk_ = [("PP%da" % b2, s_, i), ("PP%db" % b2, s_, i)] if j < 5 else [("PP%da" % b2, s_, i)]
                    rk_ = [("tb", s_, i, 0), ("tb", s_, i, 1)] if j < 5 else [("tb", s_, i, 0)]
                    if j < 5:
                        if i % 2 == 0:
                            ACT(PP[s_][b2][i][:, 0:128], B_[:, 0:128], AF.Copy, rk_[0:1], wk_[0:1])
                            CP(PP[s_][b2][i][:, 128:256], B_[:, 128:256], rk_[1:2], wk_[1:2])
                        else:
                            CP(PP[s_][b2][i][:, 0:128], B_[:, 0:128], rk_[0:1], wk_[0:1])
                            ACT(PP[s_][b2][i][:, 128:256], B_[:, 128:256], AF.Copy, rk_[1:2], wk_[1:2])
                    elif i % 2 == 0:
                        ACT(PP[s_][b2][i][:, 0:n_], B_[:, 0:n_], AF.Copy, rk_, wk_)
                    else:
                        CP(PP[s_][b2][i][:, 0:n_], B_[:, 0:n_], rk_, wk_)
                for i, t in enumerate(tl):
                    B_ = tb[i]
                    MM(B_[:, 256:384], PP[s_][b2][i][:, 0:128], TTm[s_][a][i][:], [("PP%da" % b2, s_, i), ("TT%d" % a, s_, i)],
                       [("tb", s_, i, 2)])
                for i, t in enumerate(tl):
                    B_ = tb[i]
                    TT(TTm[s_][b2][i][:], B_[:, 256:384], TTm[s_][a][i][:], ALU.add, [("tb", s_, i, 2), ("TT%d" % a, s_, i)],
                       [("TT%d" % b2, s_, i)])

            for i, t in enumerate(tl):
                pk_ = 4 * s_ + i
                TR(ptr[:, pk_ * 128:(pk_ + 1) * 128], TTm[s_][1][i][:], identb[:], [("TT1", s_, i), "identb"], [("ptr", pk_)])
                MM(tb[i][:, 0:128], Cb[s_][i][:], TTm[s_][1][i][:], [("Cb", s_, i), ("TT1", s_, i)], [("tb", s_, i, 0)])
            for i, t in enumerate(tl):
                pk_ = 4 * s_ + i
                CP(PP[s_][0][i][:, 0:128], ptr[:, pk_ * 128:(pk_ + 1) * 128], [("ptr", pk_)], [("PP0a", s_, i)])
                ACT(PP[s_][0][i][:, 128:256], tb[i][:, 0:128], AF.Copy, [("tb", s_, i, 0)], [("PP0b", s_, i)])
            for i, t in enumerate(tl):
                MM(tb[i][:, 128:256], PP[s_][0][i][:, 0:128], PP[s_][0][i][:, 128:256], [("PP0a", s_, i), ("PP0b", s_, i)],
                   [("tb", s_, i, 1)])
            for i, t in enumerate(tl):
                TT(TTm[s_][0][i][:], tb[i][:, 128:256], TTm[s_][1][i][:], ALU.add, [("tb", s_, i, 1), ("TT1", s_, i)],
                   [("TT0", s_, i)])

        def stageB(hd, b, full, par):
            kT = kTb[par]; qT = qTb[par]
            Skey = ("S", hd)
            s_ = b % 2
            tl = list(range(b * NB, b * NB + NB))
            for i, t in enumerate(tl):
                c0 = t * 128
                gk = gkeys(t)
                TTf = TTm[s_][0][i]
                MM(psn[:, 0:128], kT[:, c0:c0 + 128], Sbf[:], [("kT", par), "Sbf"], ["ps_ks"])
                if full:
                    MM(psn[:, 384:512], qT[:, c0:c0 + 128], Sbf[:], [("qT", par), "Sbf"], ["ps_qs"])
                STT(tmpb[:], psn[:, 0:128], nbegs[:, t, hd:hd + 1], vbt[s_][i][:], ALU.mult, ALU.add,
                    ["ps_ks", ("vbt", s_, i)] + gk, ["tmpb"])
                MM(psn[:, 128:256], TTf[:], tmpb[:], [("TT0", s_, i), "tmpb"], ["ps_vn"])
                ACT(vnew[:], psn[:, 128:256], AF.Copy, ["ps_vn"], ["vnew"])
                MM(psn[:, 256:384], ktl[s_][i][:], vnew[:], [("ktl", s_, i), "vnew"], ["ps_sd"])
                STT(Sbf[:], Sst[:, hd, :], gls[:, t, 0, hd:hd + 1], psn[:, 256:384], ALU.mult, ALU.add,
                    [Skey, "ps_sd"] + gk, ["Sbf"])
                STT(Sst[:, hd, :], Sst[:, hd, :], gls[:, t, 0, hd:hd + 1], psn[:, 256:384], ALU.mult, ALU.add,
                    [Skey, "ps_sd"] + gk, [Skey])
                if full:
                    MM(px[:, 128:256], QKd[s_][i][:], vnew[:], [("QKd", s_, i), "vnew"], ["px_qkv"])
                    ACT(Bc[:], px[:, 128:256], AF.Copy, ["px_qkv"], ["Bc"])
                    STT(Oh[:, t, :], psn[:, 384:512], egs[:, t, hd:hd + 1], Bc[:], ALU.mult, ALU.add,
                        ["ps_qs", "Bc"] + gk, [("Oh", t)])
                    ACT(junk[:, 0:128], Oh[:, t, :], AF.Square, [("Oh", t)], ["junk", ("ss", t)], accum_out=stat2[:, 4 + t:5 + t])

        def capture(fn, *args):
            n0 = len(P.ins)
            fn(*args)
            seg = P.ins[n0:]
            del P.ins[n0:]
            return seg

        def interleave(segB, segA):
            out = []
            ia, nA, nB_ = 0, len(segA), len(segB)
            for ib, op in enumerate(segB):
                out.append(op)
                tgt = (ib + 1) * nA // nB_
                while ia < tgt:
                    out.append(segA[ia])
                    ia += 1
            out.extend(segA[ia:])
            return out

        def gdn_head(hd, ntiles, full, par):
            szT = szTb[par]
            nb = ntiles // NB
            stageA(hd, 0, full, par)
            for b in range(1, nb):
                segA = capture(stageA, hd, b, full, par)
                segB = capture(stageB, hd, b - 1, full, par)
                P.ins.extend(interleave(segA, segB))
            stageB(hd, nb - 1, full, par)
            if full:
                ssk = [("ss", t) for t in range(ntiles)]
                TS(stat2[:, 4:12], stat2[:, 4:12], 1.0 / 128, EPS, ALU.mult, ALU.add, ssk, ["rs1"])
                ACT(stat2[:, 4:12], stat2[:, 4:12], AF.Sqrt, ["rs1"], ["rs2"])
                RECIP(stat2[:, 4:12], stat2[:, 4:12], ["rs2"], ["rs3"])
                for t in range(ntiles):
                    c0 = t * 128
                    ACT(onb[:], Oh[:, t, :], AF.Copy, [("Oh", t), "rs3"], ["onb"], scale=stat2[:, 4 + t:5 + t])
                    TR(ptr[:, 896:1024], onb[:], identb[:], ["onb", "identb"], [("ptr", 7)])
                    STT(gdnT[:, hd, c0:c0 + 128], ptr[:, 896:1024], gnw[:, 0:1], szT[:, c0:c0 + 128], ALU.mult, ALU.mult,
                        [("ptr", 7), "gnw", ("szT", par), "rs3"], [("gdnT", hd)])

        def head_load(hd, full):
            blocks = [QKV_OFF + hd * 128, QKV_OFF + 2048 + hd * 128, QKV_OFF + 4096 + hd * 128]
            if full:
                blocks.append(Z_OFF + hd * 128)
            s_ = wslot_i[0] % NWB
            wslot_i[0] += 1
            v4 = wb[s_][:, 0:KC * 512].rearrange("p (k c) -> p k c", k=KC)
            for bi, c0 in enumerate(blocks):
                P.dma("pool", "wb%d_%d" % (s_, bi), lambda h, bi=bi, c0=c0: h.dma_start(
                    out=v4[:, :, bi * 128:(bi + 1) * 128], in_=wrows("w_in", 0, KC, c0, 128)), (),
                    wkeys(s_) if bi == 0 else [("wb", s_, bi)])
            if not full:
                P.dma("pool", "wb%d_3" % s_, lambda h: h.dma_start(
                    out=v4[:, 0:1, 384:512], in_=wrows("w_in", 0, 1, Z_OFF, 128)), (), [("wb", s_, 3)])
            return v4, s_

        def head_qkv(hd, v4, s_, groups, n, hk, full, par):
            for bi, nm in enumerate(("q", "k", "v", "z")[0:4 if full else 3]):
                wk = [("wb", s_, bi)]
                if nm == "z":
                    def consz(psv, pk, g0, gn):
                        if g0 >= NT:
                            ACT(szS[:, hd, :], psv, AF.Silu, [pk], [("szS", hd)])
                        else:
                            ACT(szTb[par][:, g0:g0 + gn], psv, AF.Silu, [pk], [("szT", par)])
                    proj_chunk(v4, wk, bi * 128, groups, hk, consz)
                    continue
                chunk = {"q": 0, "k": 16, "v": 32}[nm] + hd
                if nm == "q" and not full:
                    def consq(psv, pk, g0, gn):
                        CP(pre[:, 3 + g0:3 + g0 + gn], psv, [pk], ["pre"])
                    proj_chunk(v4, wk, bi * 128, ((896, 128),), hk, consq)
                    CP(convtail[:, chunk, :], pre[:, n:n + 3], ["pre"], [("convtail", chunk)])
                    continue

                def cons(psv, pk, g0, gn, chunk=chunk):
                    if g0 >= NT:
                        CP(preS[:, chunk, :], psv, [pk], [("preS", chunk)])
                    elif g0 == 0:
                        ACT(pre[:, 3 + g0:3 + g0 + gn], psv, AF.Copy, [pk], ["pre"])
                    else:
                        CP(pre[:, 3 + g0:3 + g0 + gn], psv, [pk], ["pre"])
                if full:
                    CP(pre[:, 0:3], convtail[:, chunk, :], [("convtail", chunk)], ["pre"])
                else:
                    MEMSET(pre[:, 0:3], 0.0, ["pre"])
                proj_chunk(v4, wk, bi * 128, groups, hk, cons)
                CP(convtail[:, chunk, :], pre[:, n:n + 3], ["pre"], [("convtail", chunk)])
                dst, l2 = {"q": (qTb[par], 128.0 ** -0.5), "k": (kTb[par], 1.0), "v": (vTb[par], None)}[nm]
                conv_chunk(chunk, n, dst, (nm + "T", par), l2)

        def _main():
            for t in range(8):
                xt, xk = load_x_tile(dr["xp"][t * 128:(t + 1) * 128, :], 128)
                norm_tile(xt, 128, t, t * 128, [xk], 0)
            barrier()
            ckpt("p0norm")
            hk_prev = hkeys(range(8))
            for t in range(8):
                ab_tile(t, t * 128, 128)
            ckpt("p0ab")
            MEMSET(Sst[:, :, :], 0.0, [("S", hd) for hd in range(16)])
            for blk in range(2):
                wv, wk = wload(wrows("w_in", 0, KC, POOL_OFF + blk * 512, 512), KC, 512)
                for cc in range(4):
                    ch = blk * 4 + cc
                    ps_, pk = next_pm()
                    for k in range(KC):
                        MM(ps_[:, 0:128], wv[:, k, cc * 128:(cc + 1) * 128], hT[:, k, 896:1024], hkeys([7]) + wk, [pk],
                           start=(k == 0), stop=(k == KC - 1))
                    CP(pooltail[:, ch, :], ps_[:, 113:128], [pk], [("pooltail", ch)])
            ckpt("p0tail")
            def heads_pipeline(groups, hk, full):
                slots = {0: head_load(0, full)}
                head_qkv(0, slots[0][0], slots[0][1], groups, NT, hk, full, 0)
                slots[1] = head_load(1, full)
                for hd in range(16):
                    par = hd % 2

                    def gdn_all(hd=hd, par=par):
                        ACT(Sbf[:], Sst[:, hd, :], AF.Copy, [("S", hd)], ["Sbf"])
                        gdn_head(hd, 8, full, par)
                    segG = capture(gdn_all)
                    if hd + 1 < 16:
                        v4, s_ = slots[hd + 1]
                        segP = capture(head_qkv, hd + 1, v4, s_, groups, NT, hk, full, 1 - par)
                        P.ins.extend(interleave(segG, segP))
                        if hd + 2 < 16:
                            slots[hd + 2] = head_load(hd + 2, full)
                    else:
                        P.ins.extend(segG)
                    if full:
                        ST("o_ssm", dr["ssm_o"][hd], Sst[:, hd, :], [("S", hd)])
            heads_pipeline(TGP, hk_prev, False)
            ckpt("p0")
            dump("S0", Sst[:, :, :], [("S", hd) for hd in range(16)])
            barrier()

            for t in range(8):
                xt, xk = load_x_tile(dr["xo"][t * 128:(t + 1) * 128, :], 128)
                norm_tile(xt, 128, t, t * 128, [xk], 0)
            xt, xk = load_x_tile(dr["xs"], NS)
            norm_tile(xt, NS, 8, NT, [xk], 0)
            barrier()
            hk_own = hkeys(range(9))
            dump("hT", hT[:, :, :], hk_own)
            for t in range(8):
                ab_tile(t, t * 128, 128)
            ab_tile(8, NT, NS)
            dump("gtok", gtok[:, :, :], [("gtok", t) for t in range(9)])
            dump("btok", btok[:, :, :], [("btok", t) for t in range(9)])
            dump("gcs", gcs[:, :, :], [("gcs", t) for t in range(8)])

            ckpt("p1ab")
            ST("o_psr", dr["pool_s_rows"], dr["st_pool"][:, 1:15, :], ())
            L = NT + 15
            MEMSET(s1[:, :], 0.0, ["s1"])
            MEMSET(s2[:, :], 0.0, ["s2"])
            for gi in range(4):
                w = WINS[gi]
                wv, wk = wload(wrows("w_in", 0, KC, POOL_OFF + gi * 256, 256), KC, 256)
                for cc in range(2):
                    ch = gi * 2 + cc
                    CP(Uext[:, 0:15], pooltail[:, ch, :], [("pooltail", ch)], ["Uext"])
                    for hf in range(2):
                        LD("stp", stp[0:120, hf, :], dr["st_pool"][hf * 8:(hf + 1) * 8, :, ch * 128:(ch + 1) * 128].rearrange(
                            "i r c -> (i r) c"), ["stp"])

                    def consp(psv, pk, g0, gn, ch=ch):
                        if g0 >= NT:
                            CP(upool_s[:, ch, :], psv, [pk], [("upool_s", ch)])
                        else:
                            ACT(Uext[:, 15 + g0:15 + g0 + gn], psv, AF.Copy, [pk], ["Uext"])
                    proj_chunk(wv, wk, cc * 128, TG, hk_own, consp)
                    src = Uext
                    bufs = [s1, s2]
                    sh = 1
                    bi = 0
                    while sh < w:
                        dstb = bufs[bi % 2]
                        TT(dstb[:, sh:L], src[:, sh:L], src[:, 0:L - sh], ALU.add, ["Uext", "s1", "s2"], ["s%d" % (bi % 2 + 1)])
                        src = dstb
                        sh *= 2
                        bi += 1
                    STT(DT[:, cc, 0:NT], src[:, 15:15 + NT], 1.0 / w, Uext[:, 15:15 + NT], ALU.mult, ALU.subtract,
                        ["s1", "s2", "Uext"], [("DT", cc)])
                    TT(stat2[:, 0:16], src[:, 15:31], invcnt[:, gi * 16:(gi + 1) * 16], ALU.mult, ["s1", "s2", "invcnt"], ["pfix"])
                    TT(DT[:, cc, 0:16], stat2[:, 0:16], Uext[:, 15:31], ALU.subtract, ["pfix", "Uext", ("DT", cc)], [("DT", cc)])
                    ST("o_pt%d" % ch, dr["poolT_o"][ch * 128:(ch + 1) * 128, :], Uext[:, NT:NT + 15], ["Uext"])
                    for hf in range(2):
                        MM(px[:, 64:80], stp[0:120, hf, :], selp[:, (gi * 2 + hf) * 16:(gi * 2 + hf) * 16 + 16],
                           ["stp", "selp"], ["px_sp"], start=(hf == 0), stop=(hf == 1))
                    TT(sm16[:, 0, 0:16], px[:, 64:80], upool_s[:, ch, :], ALU.add, ["px_sp", ("upool_s", ch)], ["sm0"])
                    STT(DT[:, cc, NT:NTOK], sm16[:, 0, 0:16], 1.0 / w, upool_s[:, ch, :], ALU.mult, ALU.subtract,
                        ["sm0", ("upool_s", ch)], [("DT", cc)])
                    ST("o_psl", dr["pool_s_lastT"][ch * 128:(ch + 1) * 128, :], upool_s[:, ch, :], [("upool_s", ch)])
                if gi == 0:
                    dump("DT0", DT[:, :, :], [("DT", 0), ("DT", 1)])
                wv, wk = wload(dr["pool_w"][0, gi].rearrange("(k p) c -> p k c", p=128), 2, 256)
                for oc in range(2):
                    ch = gi * 2 + oc
                    for (g0, gn) in TG:
                        ps_, pk = next_pm()
                        for k in range(2):
                            MM(ps_[:, 0:gn], wv[:, k, oc * 128:(oc + 1) * 128], DT[:, k, g0:g0 + gn],
                               wk + [("DT", 0), ("DT", 1)], [pk], start=(k == 0), stop=(k == 1))
                        ACT(PO[:, ch, g0:g0 + gn], ps_[:, 0:gn], AF.Copy, [pk, "pscale"], [("PO", ch)], scale=pscale[:, ch:ch + 1])
            dump("PO", PO[:, :, :], [("PO", c) for c in range(8)])
            ckpt("pool")
            pok = [("PO", c) for c in range(8)]
            for jb in range(8):
                wg, wgk, wu, wuk = wload2(wrows("w_in", 0, KC, GP_OFF + jb * 256, 256), KC, 256,
                                          wrows("w_pool_up", 0, 8, jb * 256, 256), 8, 256)
                for jj in range(2):
                    j = jb * 2 + jj
                    for (g0, gn) in TG:
                        ps_, pk = next_pm()
                        for k in range(KC):
                            MM(ps_[:, 0:gn], wg[:, k, jj * 128:(jj + 1) * 128], hT[:, k, g0:g0 + gn], hk_own + wgk, [pk],
                               start=(k == 0), stop=(k == KC - 1))
                        ACT(junk[:, 0:gn], ps_[:, 0:gn], AF.Sigmoid, [pk], ["gsig"])
                        ps2, pk2 = next_pm()
                        for k in range(8):
                            MM(ps2[:, 0:gn], wu[:, k, jj * 128:(jj + 1) * 128], PO[:, k, g0:g0 + gn], pok + wuk, [pk2],
                               start=(k == 0), stop=(k == 7))
                        TT(bufM[:, j, g0:g0 + gn], ps2[:, 0:gn], junk[:, 0:gn], ALU.mult, [pk2, "gsig"], [("bufM", j)])
            dump("GA", bufM[:, :, :], [("bufM", j) for j in range(16)])
            barrier()

            ckpt("GA")
            heads_pipeline(TG, hk_own, True)
            for ch in range(48):
                ST("o_ct", dr["convT_o"][ch * 128:(ch + 1) * 128, :], convtail[:, ch, :], [("convtail", ch)])
            barrier()

            ckpt("heads1")
            def sample_path():
                LD("c11", seli[0:16, :], dr["c_seli"], ["seli"])
                ST("o_csr", dr["conv_s_rows"], dr["st_conv"][:, 1:3, :], ())
                for part in range(12):
                    LD("stc", stc[0:48, :], dr["st_conv"][:, :, part * 512:(part + 1) * 512].rearrange("i j c -> (i j) c"), ["stc"])
                    for i in range(NS):
                        LD("wrep%d" % i, wrep[i * 3:(i + 1) * 3, :], dr["conv_w"][0, 0:3, part * 512:(part + 1) * 512], [("wrep", i)])
                    TT(stc[0:48, :], stc[0:48, :], wrep[0:48, :], ALU.mult, ["stc"] + [("wrep", i) for i in range(NS)], ["stc"])
                    for h4 in range(4):
                        chunk = part * 4 + h4
                        which, hd = chunk // 16, chunk % 16
                        MM(px[:, 80:96], stc[0:48, h4 * 128:(h4 + 1) * 128], selc[:, :], ["stc", "selc"], ["px_sc"])
                        STT(sm16[:, 1, 0:16], preS[:, chunk, :], cw[:, chunk, 3:4], px[:, 80:96], ALU.mult, ALU.add,
                            ["px_sc", ("preS", chunk), "cw"], ["sm1"])
                        ST("o_csl", dr["conv_s_lastT"][chunk * 128:(chunk + 1) * 128, :], preS[:, chunk, :], [("preS", chunk)])
                        if which == 2:
                            ACT(vS[:, :, hd], sm16[:, 1, 0:16], AF.Silu, ["sm1"], ["vS"])
                        else:
                            ACT(sm16[:, 2, 0:16], sm16[:, 1, 0:16], AF.Silu, ["sm1"], ["sm2"])
                            ACT(sm16[:, 3, 0:16], sm16[:, 2, 0:16], AF.Square, ["sm2"], ["sm3"])
                            MM(px[:, 96:112], onesf[:], sm16[:, 3, 0:16], ["onesf", "sm3"], ["px_ss"])
                            ACT(sm16[:, 4, 0:16], px[:, 96:112], AF.Sqrt, ["px_ss"], ["sm4"], bias=EPS)
                            RECIP(sm16[:, 4, 0:16], sm16[:, 4, 0:16], ["sm4"], ["sm4"])
                            sc = 128.0 ** -0.5 if which == 0 else 1.0
                            STT(qkS[:, hd, :, 1 - which], sm16[:, 2, 0:16], sc, sm16[:, 4, 0:16], ALU.mult, ALU.mult,
                                ["sm2", "sm4"], ["qkS"])
                dump("qkS", qkS[:, :, :, :], ["qkS"])
                dump("vS", vS[:, :, :], ["vS"])
                for idx, src in enumerate((btok, egs)):
                    TT(xd[0:16, :, :], seli[0:16, :].rearrange("a (i e) -> a i e", i=16),
                       src[0:16, 8, :].unsqueeze(1).to_broadcast([16, 16, 16]), ALU.mult, ["seli", ("btok", 8), ("egs", 8)], ["xd"])
                    MM(px[:, 256:512], onesf[0:16, :], xd[0:16, :, :].rearrange("a i h -> a (i h)"), ["onesf", "xd"], ["px_bc"])
                    CP(bcS[:, idx, :], px[:, 256:512], ["px_bc"], [("bcS", idx)])
                TT(smpW[:, 3, :].rearrange("p (h i) -> p h i", h=16), qkS[:, :, :, 0], qkS[:, :, :, 1], ALU.mult, ["qkS"], ["smpW3"])
                MM(px[:, 256:512], onesf[:], smpW[:, 3, :], ["onesf", "smpW3"], ["px_bc"])
                CP(bcS[:, 2, :].rearrange("p (i h) -> p h i", i=16), px[:, 256:512].rearrange("p (h i) -> p h i", h=16),
                   ["px_bc"], [("bcS", 2)])
                bk = [("bcS", 0), ("bcS", 1), ("bcS", 2)]
                def ld_state(i):
                    LD("ssm%d" % (i % 2), Ssm[i % 2][:, :, :], dr["st_ssm"][i].rearrange("h d e -> d h e"),
                       [("Ssm", i % 2, hd) for hd in range(16)])
                ld_state(0)
                for i in range(NS):
                    Sb = Ssm[i % 2]
                    sk = ("Ssm", i % 2)
                    skh = [(sk[0], sk[1], hd) for hd in range(16)]
                    if i + 1 < NS:
                        ld_state(i + 1)
                    for hd in range(16):
                        MM(px[:, 256 + hd * 2:258 + hd * 2], Sb[:, hd, :], qkS[:, hd, i, :], [skh[hd], "qkS"], ["px_bc"])
                    pv = px[:, 256:288].rearrange("p (h two) -> p h two", two=2)
                    be = bcS[:, 0, i * 16:(i + 1) * 16]
                    eg_ = bcS[:, 1, i * 16:(i + 1) * 16]
                    qk_ = bcS[:, 2, i * 16:(i + 1) * 16]
                    TT(sm16[:, 6, 0:16], pv[:, :, 0], eg_, ALU.mult, ["px_bc"] + bk, ["sm6"])
                    TT(sm16[:, 6, 0:16], vS[:, i, :], sm16[:, 6, 0:16], ALU.subtract, ["vS", "sm6"], ["sm6"])
                    TT(sm16[:, 6, 0:16], sm16[:, 6, 0:16], be, ALU.mult, ["sm6"] + bk, ["sm6"])
                    TT(sm16[:, 7, 0:16], pv[:, :, 1], eg_, ALU.mult, ["px_bc"] + bk, ["sm7"])
                    TT(sm16[:, 7, 16:32], sm16[:, 6, 0:16], qk_, ALU.mult, ["sm6"] + bk, ["sm7b"])
                    TT(oS[:, i, :], sm16[:, 7, 0:16], sm16[:, 7, 16:32], ALU.add, ["sm7", "sm7b"], ["oS"])
                    for hd in range(16):
                        vcol = sm16[:, 6, hd:hd + 1].to_broadcast([128, 128])
                        pb_ = (pq, pg, psn)[hd % 3]
                        st_ = Stm2[hd % 2]
                        MM(pb_[:, 384:512], vcol, ident[:], ["sm6", "ident"], [("vbb", hd % 3)])
                        ACT(st_[:], Sb[:, hd, :], AF.Copy, [skh[hd]] + bk, [("Stm", hd % 2)],
                            scale=bcS[:, 1, i * 16 + hd:i * 16 + hd + 1])
                        STT(Sb[:, hd, :], pb_[:, 384:512], qkS[:, hd, i, 0:1], st_[:], ALU.mult, ALU.add,
                            [("vbb", hd % 3), ("Stm", hd % 2), "qkS"], [skh[hd]])
                    ST("o_ssms%d" % (i % 2), dr["ssm_s"][i].rearrange("h d e -> d h e"), Sb[:, :, :], skh)
                dump("oS", oS[:, :, :], ["oS"])
                ACT(smpW[:, 0, :], oS[:, :, :].rearrange("p i h -> p (i h)"), AF.Square, ["oS"], ["smpW0"])
                MM(px[:, 256:512], onesf[:], smpW[:, 0, :], ["onesf", "smpW0"], ["px_bc"])
                TS(smpW[:, 1, :], px[:, 256:512], 1.0 / 128, EPS, ALU.mult, ALU.add, ["px_bc"], ["smpW1"])
                ACT(smpW[:, 1, :], smpW[:, 1, :], AF.Sqrt, ["smpW1"], ["smpW1"])
                RECIP(smpW[:, 1, :], smpW[:, 1, :], ["smpW1"], ["smpW1"])
                STT(smpW[:, 2, :], oS[:, :, :].rearrange("p i h -> p (i h)"), gnw[:, 0:1], smpW[:, 1, :], ALU.mult, ALU.mult,
                    ["oS", "smpW1", "gnw"], ["smpW2"])
                sgate = smpW[:, 2, :].rearrange("p (i h) -> p i h", i=16)
                for hd in range(16):
                    TT(gdnT[:, hd, NT:NTOK], sgate[:, :, hd], szS[:, hd, :], ALU.mult, ["smpW2", ("szS", hd)], [("gdnTs", hd)])
                dump("gdnT", gdnT[:, :, :], [("gdnT", hd) for hd in range(16)] + [("gdnTs", hd) for hd in range(16)])


            def merge_pass(groups, gkn):
                gk_all = [(gkn, hd) for hd in range(16)]
                for jb in range(8):
                    wg, wgk, wu, wuk = wload2(wrows("w_in", 0, KC, GG_OFF + jb * 256, 256), KC, 256,
                                              wrows("w_gdn_up", 0, KC, jb * 256, 256), KC, 256)
                    for jj in range(2):
                        j = jb * 2 + jj
                        for (g0, gn) in groups:
                            ps_, pk = next_pm()
                            for k in range(KC):
                                MM(ps_[:, 0:gn], wg[:, k, jj * 128:(jj + 1) * 128], hT[:, k, g0:g0 + gn], hk_own + wgk, [pk],
                                   start=(k == 0), stop=(k == KC - 1))
                            ACT(junk[:, 0:gn], ps_[:, 0:gn], AF.Sigmoid, [pk], ["gsig"])
                            ps2, pk2 = next_pm()
                            for k in range(KC):
                                MM(ps2[:, 0:gn], wu[:, k, jj * 128:(jj + 1) * 128], gdnT[:, k, g0:g0 + gn], gk_all + wuk, [pk2],
                                   start=(k == 0), stop=(k == KC - 1))
                            TT(junk[:, 512:512 + gn], ps2[:, 0:gn], junk[:, 0:gn], ALU.mult, [pk2, "gsig"], ["gmul"])
                            TT(bufM[:, j, g0:g0 + gn], bufM[:, j, g0:g0 + gn], junk[:, 512:512 + gn], ALU.add,
                               ["gmul", ("bufM", j)], [("bufM", j)])


            segS = capture(sample_path)
            segM = capture(merge_pass, TG[0:2], "gdnT")
            P.ins.extend(interleave(segS, segM))
            ckpt("sample")
            merge_pass(TG[2:3], "gdnTs")
            dump("merged", bufM[:, :, :], [("bufM", j) for j in range(16)])
            ckpt("merge")
            all_keys = set()
            for I in P.ins:
                all_keys.update(I.reads)
                all_keys.update(I.writes)
            all_keys = list(all_keys) + list(UNION_KEYS)
            mk = [("bufM", j) for j in range(16)]
            tiles = [(t, 128, dr["xo"][t * 128:(t + 1) * 128, :], dr["y_o"][t * 128:(t + 1) * 128, :], t * 128) for t in range(8)]
            tiles.append((8, NS, dr["xs"], dr["y_s"], NT))
            for (t, npart, src, _, _) in tiles:
                LD("xr%d" % t, xres[0:npart, t, :], src, [("xres", t)] + (all_keys if t == 0 else []))
            for cg in range(4):
                wv, wk = wload(wrows("w_o", 0, KC, cg * 512, 512), KC, 512)
                for (t, npart, _, _, tc0) in tiles:
                    ps_, pk = next_pm()
                    for k in range(KC):
                        MM(ps_[0:npart, :], bufM[:, k, tc0:tc0 + npart], wv[:, k, :], mk + wk, [pk], start=(k == 0), stop=(k == KC - 1))
                    TT(xres[0:npart, t, cg * 512:(cg + 1) * 512], xres[0:npart, t, cg * 512:(cg + 1) * 512], ps_[0:npart, :], ALU.add,
                       [pk, ("xres", t)], [("xres", t)])
            dump("x1", xres[:, 0:8, :], [("xres", t) for t in range(8)])
            for (t, npart, _, _, tc0) in tiles:
                norm_tile(xres[0:npart, t, :], npart, t, tc0, [("xres", t)], 1)
            ckpt("wo")
            for kg in range(4):
                cbase = kg * 1408
                done = 0
                for nblk in (256, 256, 256, 256, 256, 128):
                    wg, wgk, wu, wuk = wload2(wrows("w_gate_up", 0, KC, cbase + done, nblk), KC, nblk,
                                              wrows("w_gate_up", 0, KC, DFF + cbase + done, nblk), KC, nblk)
                    for cc in range(nblk // 128):
                        a = (done // 128) + cc
                        for (g0, gn) in TG:
                            ps_, pk = next_pm()
                            for k in range(KC):
                                MM(ps_[:, 0:gn], wg[:, k, cc * 128:(cc + 1) * 128], hT[:, k, g0:g0 + gn], hk_own + wgk, [pk],
                                   start=(k == 0), stop=(k == KC - 1))
                            ACT(xn[:, 0:gn], ps_[:, 0:gn], AF.Silu, [pk], ["xn"])
                            ps2, pk2 = next_pm()
                            for k in range(KC):
                                MM(ps2[:, 0:gn], wu[:, k, cc * 128:(cc + 1) * 128], hT[:, k, g0:g0 + gn], hk_own + wuk, [pk2],
                                   start=(k == 0), stop=(k == KC - 1))
                            TT(bufM[:, a, g0:g0 + gn], ps2[:, 0:gn], xn[:, 0:gn], ALU.mult, [pk2, "xn"], [("bufM", a)])
                    done += nblk
                ak = [("bufM", a) for a in range(11)]
                for cg in range(4):
                    wv, wk = wload(wrows("w_down", cbase, 11, cg * 512, 512), 11, 512)
                    for (t, npart, _, _, tc0) in tiles:
                        ps_, pk = next_pm()
                        for k in range(11):
                            MM(ps_[0:npart, :], bufM[:, k, tc0:tc0 + npart], wv[:, k, :], ak + wk, [pk], start=(k == 0), stop=(k == 10))
                        TT(xres[0:npart, t, cg * 512:(cg + 1) * 512], xres[0:npart, t, cg * 512:(cg + 1) * 512], ps_[0:npart, :],
                           ALU.add, [pk, ("xres", t)], [("xres", t)])
            dump("x2", xres[:, 0:8, :], [("xres", t) for t in range(8)])
            ckpt("ffn")
            for (t, npart, _, _, tc0) in tiles:
                CP(xn[0:npart, :], xres[0:npart, t, :], [("xres", t)], ["xn"])
                transposes_to_hT(npart, t, tc0, None)
            ptk = [("bufM", 12), ("bufM", 13)]
            for (t, npart, _, _, tc0) in tiles:
                psrc = dr["po"][t * 128:(t + 1) * 128, :] if t < 8 else dr["psm"]
                LD("pin", pin[0:npart, :], psrc, ["pin"])
                CP(xn[0:npart, 0:256], pin[0:npart, :], ["pin"], ["xn"])
                for kk in range(2):
                    TR(ptr[:, kk * 128:kk * 128 + npart], xn[0:npart, kk * 128:(kk + 1) * 128], identb[0:npart, 0:npart],
                       ["xn", "identb"], [("ptr", kk)])
                ACT(pT[:, :, tc0:tc0 + npart], ptr[:, 0:256].rearrange("p (k t) -> p k t", k=2)[:, :, 0:npart], AF.Copy,
                    [("ptr", 0), ("ptr", 1)] + ptk, ptk)
            for cg in range(8):
                wgv, wgk, wpv, wpk = wload2(wrows("w_ple_gate", 0, KC, cg * 256, 256), KC, 256,
                                            wrows("w_ple", 0, 2, cg * 256, 256), 2, 256)
                for (t, npart, _, _, tc0) in tiles:
                    ps_, pk = next_pm()
                    for k in range(KC):
                        MM(ps_[0:npart, 0:256], hT[:, k, tc0:tc0 + npart], wgv[:, k, :], hk_own + wgk, [pk], start=(k == 0), stop=(k == KC - 1))
                    ACT(xn[0:npart, 0:256], ps_[0:npart, 0:256], AF.Sigmoid, [pk], ["xn"])
                    ps2, pk2 = next_pm()
                    for k in range(2):
                        MM(ps2[0:npart, 0:256], pT[:, k, tc0:tc0 + npart], wpv[:, k, :], ptk + wpk, [pk2], start=(k == 0), stop=(k == 1))
                    TT(xn[0:npart, 512:768], ps2[0:npart, 0:256], xn[0:npart, 0:256], ALU.mult, [pk2, "xn"], ["xnb"])
                    TT(xres[0:npart, t, cg * 256:(cg + 1) * 256], xres[0:npart, t, cg * 256:(cg + 1) * 256], xn[0:npart, 512:768],
                       ALU.add, ["xnb", ("xres", t)], [("xres", t)])
            ckpt("ple")
            nfrow = hT[:, :, :].rearrange("p k t -> p (k t)")[:, 0:4096].bitcast(F32)
            LD("nf", nfrow, dr["norm_final"].rearrange("(o d) -> o d", o=1).partition_broadcast(128), hk_own)
            for (t, npart, _, ydst, tc0) in tiles:
                xt = xres[0:npart, t, :]
                rstd_of(xt, npart, [("xres", t)])
                STT(xt, xt, stat[0:npart, 3:4], nfrow[0:npart, :], ALU.mult, ALU.mult, [("xres", t), "stat3"] + hk_own, [("xres", t)])
                ST("o_y", ydst, xt, [("xres", t)])

        try:
            _main()
        except _Stop:
            pass
        P.emit()
    return nc, dbg_out


def _consts():
    c = {}
    c["c_ident"] = np.eye(128, dtype=np.float32)
    r = np.arange(128)
    blk = (r[:, None] // 64) == (r[None, :] // 64)
    c["c_utri"] = (r[:, None] <= r[None, :]).astype(np.float32)
    c["c_ma"] = np.where((r[None, :] >= r[:, None]) | ~blk, BIG, 0.0).astype(np.float32)
    c["c_mb"] = np.where(r[None, :] < r[:, None], -BIG, 0.0).astype(np.float32)
    hm = np.zeros((128, 256), np.float32)
    hm[:, 0:128] = np.where((r[:, None] >= 64) & (r[None, :] < 64), 0.0, BIG)
    c["c_hm"] = hm
    c["c_bo"] = blk.astype(np.float32)
    selp = np.zeros((120, 4, 2, 16), np.float32)
    for gi, w in enumerate(WINS):
        for hf in range(2):
            for il in range(8):
                for rr in range(15):
                    if rr >= 15 - (w - 1):
                        selp[il * 15 + rr, gi, hf, hf * 8 + il] = 1.0
    c["c_selp"] = selp.reshape(120, 128)
    selc = np.zeros((48, 16), np.float32)
    for i in range(16):
        selc[i * 3:(i + 1) * 3, i] = 1.0
    c["c_selc"] = selc
    seli = np.zeros((16, 16, 128), np.float32)
    for i in range(16):
        seli[i, i, :] = 1.0
    c["c_seli"] = np.ascontiguousarray(seli[:, :, 0:16]).reshape(16, 256)
    return c


def _invcnt(start_pos):
    t = np.zeros((4, 16), np.float32)
    for gi, w in enumerate(WINS):
        for p in range(16):
            t[gi, p] = 1.0 / min(start_pos + p + 1, w)
    return t.reshape(1, 64)


_CACHE = {}


def make_in_maps(inp):
    consts = _consts()
    in_maps = []
    for c in range(8):
        b, hf = c // 2, c % 2
        m = {k: inp[k] for k in W_SHAPES}
        m.update(consts)
        m["xo"] = inp["x_prompt"][b, hf * NT:(hf + 1) * NT]
        m["xp"] = inp["x_prompt"][b, 0:NT] if hf == 1 else np.zeros((NT, D), np.float32)
        m["xs"] = inp["x_sample"][c * NS:(c + 1) * NS, 0]
        m["po"] = inp["p_prompt"][0, b, hf * NT:(hf + 1) * NT]
        m["psm"] = inp["p_sample"][0, c * NS:(c + 1) * NS, 0]
        m["st_pool"] = inp["state_pool"][0, c * NS:(c + 1) * NS]
        m["st_conv"] = inp["state_conv"][0, c * NS:(c + 1) * NS]
        m["st_ssm"] = inp["state_ssm"][0, c * NS:(c + 1) * NS]
        m["c_invcnt"] = _invcnt(hf * NT)
        m["l_cw"] = inp["conv_w"][0].reshape(4, 48, 128).transpose(2, 1, 0).reshape(128, 192)
        m["l_pscale"] = inp["pool_scale"][0].reshape(8, 128).T
        m["l_gnw"] = inp["gdn_norm"][0].reshape(128, 1)
        m["l_nw"] = np.concatenate([inp["norm_mix"][0].reshape(16, 128).T, inp["norm_ffn"][0].reshape(16, 128).T], axis=1)
        in_maps.append({k: np.ascontiguousarray(v) for k, v in m.items()})
    return in_maps


def kernel(**inputs):
    inp = {k: np.ascontiguousarray(np.asarray(v, dtype=np.float32)) for k, v in inputs.items()}
    if "nc" not in _CACHE:
        _CACHE["nc"] = build()[0]
    nc = _CACHE["nc"]
    in_maps = make_in_maps(inp)
    res = run_bass_kernel_spmd(nc, in_maps, core_ids=list(range(8))).results
    y_p = np.zeros((4, 2048, D), np.float32)
    y_s = np.zeros((128, 1, D), np.float32)
    pool_p = np.zeros((1, 4, 15, 1024), np.float32)
    conv_p = np.zeros((1, 4, 3, 6144), np.float32)
    ssm_p = np.zeros((1, 4, 16, 128, 128), np.float32)
    pool_s = np.zeros((1, 128, 15, 1024), np.float32)
    conv_s = np.zeros((1, 128, 3, 6144), np.float32)
    ssm_s = np.zeros((1, 128, 16, 128, 128), np.float32)
    for c in range(8):
        b, hf = c // 2, c % 2
        r = res[c]
        y_p[b, hf * NT:(hf + 1) * NT] = r["y_o"]
        y_s[c * NS:(c + 1) * NS, 0] = r["y_s"]
        if hf == 1:
            pool_p[0, b] = r["poolT_o"].T
            conv_p[0, b] = r["convT_o"].T
            ssm_p[0, b] = r["ssm_o"]
        sl = slice(c * NS, (c + 1) * NS)
        pool_s[0, sl, 0:14] = r["pool_s_rows"]
        pool_s[0, sl, 14] = r["pool_s_lastT"].T
        conv_s[0, sl, 0:2] = r["conv_s_rows"]
        conv_s[0, sl, 2] = r["conv_s_lastT"].T
        ssm_s[0, sl] = r["ssm_s"]
    return (y_p, y_s, pool_p, conv_p, ssm_p, pool_s, conv_s, ssm_s)
```
